# Optimizing a Trainium2 kernel written in Bass

```python
import jax, jax.numpy as jnp
from jax import lax
import numpy as np

D_MODEL = 1024
BATCH = 8
SEQ = 2048
DEPTH = 1

GRID_W = 64
CTX_LEN = 256
D_RNN = 1024
N_LRU_BLOCKS = 16
LRU_BLOCK = D_RNN // N_LRU_BLOCKS
LRU_C = 8.0
CONV_W = 4
CONV_PAD_LEFT = 2
N_HEADS = 16
HEAD_DIM = 64
D_ATT = N_HEADS * HEAD_DIM
WIN_ROWS = 8
WIN_COLS = 16
Q_BLOCK_COLS = 16
K_BLOCK_COLS = Q_BLOCK_COLS + WIN_COLS
D_FF = 4 * D_MODEL
N_MOD = 6
EPS = 1e-6
NEG = -1e30
D_IN = 2 * D_RNN + 3 * D_ATT + 2 * D_MODEL
SPLITS = (D_RNN, 2 * D_RNN, 2 * D_RNN + D_ATT, 2 * D_RNN + 2 * D_ATT, 2 * D_RNN + 3 * D_ATT)

kernel_name = 'hybrid_rglru_natten_dit_block'


def rms_norm(x, g):
    xf = x.astype(jnp.float32)
    y = xf * lax.rsqrt(jnp.mean(xf * xf, axis=-1, keepdims=True) + EPS)
    return (y * g.astype(jnp.float32)).astype(x.dtype)


def modulate(x, g, shift, scale):
    return rms_norm(x, g) * (1.0 + scale) + shift


def short_conv(x, w, b):
    L = x.shape[1]
    xp = jnp.pad(x, ((0, 0), (CONV_PAD_LEFT, CONV_W - 1 - CONV_PAD_LEFT), (0, 0)))
    out = b
    for k in range(CONV_W):
        out = out + w[k] * xp[:, k:k + L]
    return out


def rglru_coeffs(xc, w_rg, b_rg, lam):
    Bn, L, _ = xc.shape
    xb = xc.reshape(Bn, L, N_LRU_BLOCKS, LRU_BLOCK)
    gates = jnp.einsum('blhi,ghij->gblhj', xb, w_rg).reshape(2, Bn, L, D_RNN)
    gates = jax.nn.sigmoid(gates + b_rg[:, None, None, :])
    r, i = gates[0], gates[1]
    log_a = -LRU_C * r * jax.nn.softplus(-lam)
    a = jnp.exp(log_a)
    b = jnp.sqrt(-jnp.expm1(2.0 * log_a)) * (i * xc)
    return a, b


def _lin_combine(e, l):
    a1, b1 = e
    a2, b2 = l
    return a1 * a2, a2 * b1 + b2


def linear_scan(a, b, h0, reverse):
    idx = -1 if reverse else 0
    b = b.at[:, idx].add(a[:, idx] * h0)
    _, h = lax.associative_scan(_lin_combine, (a, b), axis=1, reverse=reverse)
    return h


def rglru_branch(xr, xr_c, conv_w, conv_b, w_rg, b_rg, lam):
    xc = short_conv(xr, conv_w, conv_b)
    xc_c = short_conv(xr_c, conv_w, conv_b)
    h_lat = jnp.zeros_like(xc)
    h_ctx = jnp.zeros_like(xc_c)
    for d, rev in enumerate((False, True)):
        a_c, b_c = rglru_coeffs(xc_c, w_rg[d], b_rg[d], lam[d])
        hc = linear_scan(a_c, b_c, jnp.zeros_like(b_c[:, 0]), rev)
        h_final = hc[:, 0] if rev else hc[:, -1]
        a_l, b_l = rglru_coeffs(xc, w_rg[d], b_rg[d], lam[d])
        h_lat = h_lat + linear_scan(a_l, b_l, h_final, rev)
        h_ctx = h_ctx + hc
    return h_lat, h_ctx


def na_attention(q, k, v, k_c, v_c, rpb):
    Bn, S = q.shape[0], q.shape[1]
    rows = S // GRID_W
    wr = min(WIN_ROWS, rows)
    n_cb = GRID_W // Q_BLOCK_COLS
    scale = HEAD_DIM ** -0.5
    qg = (q * scale).reshape(Bn, rows, GRID_W, N_HEADS, HEAD_DIM)
    kg = k.reshape(Bn, rows, GRID_W, N_HEADS, HEAD_DIM)
    vg = v.reshape(Bn, rows, GRID_W, N_HEADS, HEAD_DIM)
    q_col = np.arange(GRID_W).reshape(n_cb, Q_BLOCK_COLS)
    col_start = np.clip(q_col - WIN_COLS // 2, 0, GRID_W - WIN_COLS)
    band_start = np.clip(np.arange(n_cb) * Q_BLOCK_COLS - WIN_COLS // 2, 0, GRID_W - K_BLOCK_COLS)
    k_col = band_start[:, None] + np.arange(K_BLOCK_COLS)
    dc = k_col[:, None, :] - q_col[:, :, None]
    col_ok = (k_col[:, None, :] >= col_start[:, :, None]) & (k_col[:, None, :] < col_start[:, :, None] + WIN_COLS)
    col_bias = rpb.astype(jnp.float32)[:, :, np.clip(dc + WIN_COLS - 1, 0, 2 * WIN_COLS - 2)]
    col_bias = jnp.where(col_ok[None, None], col_bias, NEG).transpose(0, 2, 3, 1, 4)

    def row_block(r):
        q_r = lax.dynamic_index_in_dim(qg, r, axis=1, keepdims=False).reshape(Bn, n_cb, Q_BLOCK_COLS, N_HEADS, HEAD_DIM)
        r0 = jnp.clip(r - wr // 2, 0, rows - wr)
        k_blk = lax.dynamic_slice_in_dim(kg, r0, wr, axis=1)[:, :, k_col]
        v_blk = lax.dynamic_slice_in_dim(vg, r0, wr, axis=1)[:, :, k_col]
        dr = r0 + jnp.arange(wr) - r + (WIN_ROWS - 1)
        bias = jnp.take(col_bias, dr, axis=3)
        s_loc = jnp.einsum('bjqhd,bwjkhd->bhjqwk', q_r, k_blk, preferred_element_type=jnp.float32) + bias[None]
        s_loc = s_loc.reshape(Bn, N_HEADS, n_cb, Q_BLOCK_COLS, wr * K_BLOCK_COLS)
        s_ctx = jnp.einsum('bjqhd,bchd->bhjqc', q_r, k_c, preferred_element_type=jnp.float32)
        p = jax.nn.softmax(jnp.concatenate([s_loc, s_ctx], axis=-1), axis=-1)
        p_loc = p[..., :wr * K_BLOCK_COLS].reshape(Bn, N_HEADS, n_cb, Q_BLOCK_COLS, wr, K_BLOCK_COLS).astype(v.dtype)
        p_ctx = p[..., wr * K_BLOCK_COLS:].astype(v.dtype)
        o = jnp.einsum('bhjqwk,bwjkhd->bjqhd', p_loc, v_blk) + jnp.einsum('bhjqc,bchd->bjqhd', p_ctx, v_c)
        return o.reshape(Bn, GRID_W, D_ATT)

    out = lax.map(row_block, jnp.arange(rows))
    return out.transpose(1, 0, 2, 3).reshape(Bn, S, D_ATT)


def ctx_attention(q_c, k_c, v_c):
    s = jnp.einsum('bqhd,bkhd->bhqk', q_c * HEAD_DIM ** -0.5, k_c, preferred_element_type=jnp.float32)
    p = jax.nn.softmax(s, axis=-1).astype(v_c.dtype)
    o = jnp.einsum('bhqk,bkhd->bqhd', p, v_c)
    return o.reshape(o.shape[0], o.shape[1], D_ATT)


def heads(t):
    return t.reshape(t.shape[0], t.shape[1], N_HEADS, HEAD_DIM)


def token_mixer(u, u_c, w_in, b_gate, conv_w, conv_b, w_rg, b_rg, lam, rpb, w_lru_out, w_na_out, w_o, ctx_out):
    xr, gr, q, k, v, gl = jnp.split(u @ w_in, SPLITS, axis=-1)
    xr_c, gr_c, q_c, k_c, v_c, gl_c = jnp.split(u_c @ w_in, SPLITS, axis=-1)
    h_lat, h_ctx = rglru_branch(xr, xr_c, conv_w, conv_b, w_rg, b_rg, lam)
    y_lru = (h_lat * jax.nn.gelu(gr)) @ w_lru_out
    y_na = na_attention(heads(q), heads(k), heads(v), heads(k_c), heads(v_c), rpb) @ w_na_out
    g_lru, g_na = jnp.split(jax.nn.sigmoid(gl + b_gate), 2, axis=-1)
    y = (g_lru * y_lru + g_na * y_na) @ w_o
    if not ctx_out:
        return y, None
    y_lru_c = (h_ctx * jax.nn.gelu(gr_c)) @ w_lru_out
    y_na_c = ctx_attention(heads(q_c), heads(k_c), heads(v_c)) @ w_na_out
    gc_lru, gc_na = jnp.split(jax.nn.sigmoid(gl_c + b_gate), 2, axis=-1)
    y_c = (gc_lru * y_lru_c + gc_na * y_na_c) @ w_o
    return y, y_c


def squared_relu_mlp(u, w1, w2):
    return jnp.square(jax.nn.relu(u @ w1)) @ w2


def setup_inputs(seed: int = 0) -> dict:
    key = jax.random.key(seed)
    ks = jax.random.split(key, 24)
    f32 = jnp.float32

    def nrm(k, shape, fan_in, gain=1.0):
        return gain * jax.random.normal(k, shape, f32) * fan_in ** -0.5

    u = jax.random.uniform(ks[13], (DEPTH, 2, D_RNN), f32, 0.9, 0.999)
    s = u ** (1.0 / LRU_C)
    lam = jnp.log(s) - jnp.log1p(-s)
    return {
        'x': jax.random.normal(ks[0], (BATCH, SEQ, D_MODEL), f32),
        'c': jax.random.normal(ks[1], (BATCH, D_MODEL), f32),
        'ctx': jax.random.normal(ks[2], (BATCH, CTX_LEN, D_MODEL), f32),
        'c_ctx': jax.random.normal(ks[3], (D_MODEL,), f32),
        'w_ada': nrm(ks[4], (DEPTH, D_MODEL, N_MOD * D_MODEL), D_MODEL, 0.5),
        'b_ada': 0.02 * jax.random.normal(ks[5], (DEPTH, N_MOD * D_MODEL), f32),
        'g_norm': 1.0 + 0.05 * jax.random.normal(ks[6], (DEPTH, 4, D_MODEL), f32),
        'w_in': nrm(ks[7], (DEPTH, D_MODEL, D_IN), D_MODEL),
        'b_gate': 0.02 * jax.random.normal(ks[8], (DEPTH, 2 * D_MODEL), f32),
        'conv_w': nrm(ks[9], (DEPTH, CONV_W, D_RNN), CONV_W),
        'conv_b': 0.02 * jax.random.normal(ks[10], (DEPTH, D_RNN), f32),
        'w_rg': nrm(ks[11], (DEPTH, 2, 2, N_LRU_BLOCKS, LRU_BLOCK, LRU_BLOCK), LRU_BLOCK),
        'b_rg': 0.02 * jax.random.normal(ks[12], (DEPTH, 2, 2, D_RNN), f32),
        'lam': lam,
        'rpb': 0.1 * jax.random.normal(ks[14], (DEPTH, N_HEADS, 2 * WIN_ROWS - 1, 2 * WIN_COLS - 1), f32),
        'w_lru_out': nrm(ks[15], (DEPTH, D_RNN, D_MODEL), D_RNN),
        'w_na_out': nrm(ks[16], (DEPTH, D_ATT, D_MODEL), D_ATT),
        'w_o': nrm(ks[17], (DEPTH, D_MODEL, D_MODEL), D_MODEL),
        'w_mlp1': nrm(ks[18], (DEPTH, D_MODEL, D_FF), D_MODEL),
        'w_mlp2': nrm(ks[19], (DEPTH, D_FF, D_MODEL), D_FF),
    }


def reference(x, c, ctx, c_ctx, w_ada, b_ada, g_norm, w_in, b_gate, conv_w, conv_b, w_rg, b_rg, lam, rpb,
              w_lru_out, w_na_out, w_o, w_mlp1, w_mlp2):
    for l in range(DEPTH):
        update_ctx = l < DEPTH - 1
        mod = jax.nn.silu(c) @ w_ada[l] + b_ada[l]
        sh1, sc1, gt1, sh2, sc2, gt2 = [m[:, None, :] for m in jnp.split(mod, N_MOD, axis=-1)]
        mod_c = jax.nn.silu(c_ctx) @ w_ada[l] + b_ada[l]
        sh1c, sc1c, gt1c, sh2c, sc2c, gt2c = jnp.split(mod_c, N_MOD, axis=-1)
        u = modulate(x, g_norm[l, 0], sh1, sc1)
        u_c = modulate(ctx, g_norm[l, 0], sh1c, sc1c)
        y, y_c = token_mixer(u, u_c, w_in[l], b_gate[l], conv_w[l], conv_b[l], w_rg[l], b_rg[l], lam[l], rpb[l],
                             w_lru_out[l], w_na_out[l], w_o[l], update_ctx)
        x = x + gt1 * rms_norm(y, g_norm[l, 1])
        u = modulate(x, g_norm[l, 2], sh2, sc2)
        x = x + gt2 * rms_norm(squared_relu_mlp(u, w_mlp1[l], w_mlp2[l]), g_norm[l, 3])
        if update_ctx:
            ctx = ctx + gt1c * rms_norm(y_c, g_norm[l, 1])
            u_c = modulate(ctx, g_norm[l, 2], sh2c, sc2c)
            ctx = ctx + gt2c * rms_norm(squared_relu_mlp(u_c, w_mlp1[l], w_mlp2[l]), g_norm[l, 3])
    return x
```

```python
from contextlib import ExitStack
import numpy as np
import concourse.bass as bass
import concourse.mybir as mybir
from concourse.bass_utils import run_bass_kernel_spmd

F32 = mybir.dt.float32
BF16 = mybir.dt.bfloat16
AF = mybir.ActivationFunctionType
ALU = mybir.AluOpType

D = 1024
S = 2048
C = 256
LAT0 = 8
CTX0 = 2064
NTP = 2320
NL = 2304
EPS = 1e-6
NEG = -1e30
KD = 24
DBG = {"npairs": 8, "att": True, "nj": 4, "local": True, "norm": True, "kcopy": True, "vproj": True, "vtr": True, "vev": 3, "fastrec": False, "pro_skew": True, "epi_skew": True}
ENGS = ("pe", "act", "dve", "pool", "sp")


class Sched:
    def __init__(self, nc, flags=None):
        self.nc = nc
        self.emit = flags is not None
        self.flags = flags
        self.eng = {"pe": nc.tensor, "act": nc.scalar, "dve": nc.vector, "pool": nc.gpsimd, "sp": nc.sync}
        self.ids = {e: 0 for e in ENGS}
        self.sigcount = {e: 0 for e in ENGS}
        self.sigval = {e: {} for e in ENGS}
        self.need = {e: set() for e in ENGS}
        self.known = {e: {p: -1 for p in ENGS} for e in ENGS}
        self.kdma = {e: {} for e in ENGS}
        self.res = {}
        self.dma_n = 0
        self.sem = {}
        self.dsem = []
        self.n_wait = 0
        self.efree = {e: 0.0 for e in ENGS}
        self.rt_w = {}
        self.rt_r = {}
        self.dma_free = 0.0

    def est_start(self, eng, reads, writes):
        t = self.efree[eng]
        for r in reads:
            t = max(t, self.rt_w.get(r, 0.0) + 150.0)
        for w in writes:
            t = max(t, self.rt_w.get(w, 0.0) + 150.0, self.rt_r.get(w, 0.0) + 150.0)
        return t

    def account(self, eng, reads, writes, cost, is_dma=False):
        t0 = self.est_start(eng, reads, writes)
        if is_dma:
            self.efree[eng] = t0 + 100.0
            t0 = max(t0, self.dma_free)
            self.dma_free = t0 + cost
            t1 = t0 + cost + 2000.0
        else:
            t1 = t0 + cost
            self.efree[eng] = t1
        for r in reads:
            self.rt_r[r] = max(self.rt_r.get(r, 0.0), t1)
        for w in writes:
            self.rt_w[w] = t1
            self.rt_r[w] = 0.0

    def run_list(self, gens):
        pend = {}
        gens = list(gens)
        while gens:
            for g in list(gens):
                if pend.get(id(g)) is None:
                    try:
                        pend[id(g)] = next(g)
                    except StopIteration:
                        gens.remove(g)
                        pend.pop(id(g), None)
            best = None
            for g in gens:
                d = pend.get(id(g))
                if d is None:
                    continue
                t = self.est_start(d[0], d[2], d[3])
                if best is None or t < best[0]:
                    best = (t, g, d)
            if best is None:
                if gens:
                    continue_possible = False
                    for g in gens:
                        if pend.get(id(g)) is None:
                            continue_possible = True
                    if not continue_possible:
                        raise RuntimeError("list scheduler deadlock")
                    self._spin = getattr(self, "_spin", 0) + 1
                    if self._spin > 100000:
                        raise RuntimeError("list scheduler livelock")
                continue
            self._spin = 0
            _, g, d = best
            pend[id(g)] = None
            is_dma = len(d) > 5 and d[5] == "dma"
            self.account(d[0], d[2], d[3], d[4], is_dma)
            if is_dma:
                self.dma(d[0], d[1], reads=d[2], writes=d[3])
            else:
                self.op(d[0], d[1], reads=d[2], writes=d[3])

    def alloc_sems(self, stack):
        for e in ENGS:
            self.sem[e] = stack.enter_context(self.nc.semaphore("s_" + e))
        for i in range(KD):
            self.dsem.append(stack.enter_context(self.nc.semaphore("d_%d" % i)))

    def _st(self, r):
        st = self.res.get(r)
        if st is None:
            st = {"w": None, "rd": {}, "drd": []}
            self.res[r] = st
        return st

    def _wait(self, eng, dep):
        pe_, pid = dep
        if pe_ == "dma":
            slot = pid % KD
            if self.kdma[eng].get(slot, -1) >= pid:
                return
            self.kdma[eng][slot] = pid
            self.n_wait += 1
            if self.emit:
                self.eng[eng].wait_ge(self.dsem[slot], 16 * (pid // KD + 1))
        else:
            if (pe_ == eng and eng in ("pe", "sp")) or self.known[eng][pe_] >= pid:
                return
            self.known[eng][pe_] = pid
            self.n_wait += 1
            if self.emit:
                self.eng[eng].wait_ge(self.sem[pe_], self.sigval[pe_][pid])
            else:
                self.need[pe_].add(pid)

    def _deps(self, reads, writes, eng=None):
        deps = []
        for r in reads:
            st = self._st(r)
            if st["w"] is not None:
                deps.append(st["w"])
        for w in writes:
            st = self._st(w)
            if st["w"] is not None:
                deps.append(st["w"])
            deps.extend(d_ for d_ in st["rd"].items() if d_[0] != eng)
            deps.extend(("dma", n) for n in st["drd"])
        return deps

    def op(self, eng, fn, reads=(), writes=()):
        for dep in self._deps(reads, writes, eng):
            self._wait(eng, dep)
        my = self.ids[eng]
        self.ids[eng] += 1
        if self.emit:
            ins = fn()
            if my in self.flags[eng]:
                ins.then_inc(self.sem[eng], 1)
                self.sigcount[eng] += 1
                self.sigval[eng][my] = self.sigcount[eng]
        for r in reads:
            self._st(r)["rd"][eng] = my
        for w in writes:
            st = self._st(w)
            st["w"] = (eng, my)
            st["rd"] = {}
            st["drd"] = []

    def dma(self, q, fn, reads=(), writes=()):
        for dep in self._deps(reads, writes, q):
            self._wait(q, dep)
        n = self.dma_n
        self.dma_n += 1
        if n >= KD:
            self._wait(q, ("dma", n - KD))
        if self.emit:
            fn().then_inc(self.dsem[n % KD], 16)
        for r in reads:
            self._st(r)["drd"].append(n)
        for w in writes:
            st = self._st(w)
            st["w"] = ("dma", n)
            st["rd"] = {}
            st["drd"] = []

    def barrier(self, final=False):
        for e in ENGS:
            for p in ENGS:
                if p != e and self.ids[p] > 0:
                    self._wait(e, (p, self.ids[p] - 1))
            for n in range(max(0, self.dma_n - KD), self.dma_n):
                self._wait(e, ("dma", n))
        if not final:
            self.res = {}


def c_pe(n):
    return 64.0 + n / 2.0


def c_act(n):
    return 220.0 + n * 1.0


def c_dve(n, two=False):
    return 100.0 + n * (2.1 if two else 1.05)


def c_pool(n):
    return 300.0 + n * 1.9


def run_skewed(gens):
    it = iter(gens)
    active = []
    more = True
    while more or active:
        if more:
            try:
                active.append(next(it))
            except StopIteration:
                more = False
        for g in list(active):
            try:
                next(g)
            except StopIteration:
                active.remove(g)


class Arena:
    def __init__(self, big, nbytes):
        self.big = big
        self.free = [(0, nbytes)]

    def alloc(self, shape, dt_):
        esz = 4 if dt_ == F32 else 2
        n = 1
        for d_ in shape[1:]:
            n *= d_
        nb = (n * esz + 63) // 64 * 64
        for i, (o, sz) in enumerate(self.free):
            if sz >= nb:
                if sz == nb:
                    self.free.pop(i)
                else:
                    self.free[i] = (o + nb, sz - nb)
                ap = self.big[0:shape[0], o // 2:(o + n * esz) // 2]
                if dt_ == F32:
                    ap = ap.bitcast(F32)
                if len(shape) == 3:
                    ap = ap.rearrange("p (a b) -> p a b", a=shape[1], b=shape[2])
                elif len(shape) == 4:
                    ap = ap.rearrange("p (a b c) -> p a b c", a=shape[1], b=shape[2], c=shape[3])
                return ap, (o, nb)
        raise RuntimeError("arena out of SBUF: need %d, free=%s" % (nb, self.free))

    def release(self, h):
        self.free.append(h)
        self.free.sort()
        m = []
        for o, sz in self.free:
            if m and m[-1][0] + m[-1][1] == o:
                m[-1] = (m[-1][0], m[-1][1] + sz)
            else:
                m.append((o, sz))
        self.free = m


class Stage(ExitStack):
    def __init__(self, arena):
        super().__init__()
        self.arena = arena
        self.handles = []

    def __exit__(self, *a):
        for h in self.handles:
            self.arena.release(h)
        self.handles = []
        return super().__exit__(*a)

    def close(self):
        self.__exit__(None, None, None)


def program(nc, sc, stage_limit=99, dbg=False):
    PE, ACT, DVE, POOL, SP = nc.tensor, nc.scalar, nc.vector, nc.gpsimd, nc.sync

    def din(name, shape):
        return nc.dram_tensor(name, shape, F32, kind="ExternalInput").ap()

    x_d = din("x", [S, D]); ctx_d = din("ctx", [C, D]); cc2_d = din("cc2", [128, 8, 2])
    wada_d = din("w_ada", [D, 6 * D]); win_d = din("w_in", [D, 7 * D])
    wlo_d = din("w_lru_out", [D, D]); wno_d = din("w_na_out", [D, D]); wo_d = din("w_o", [D, D])
    w1_d = din("w_mlp1", [D, 4 * D]); w2_d = din("w_mlp2", [4 * D, D])
    bada_d = din("badaFM", [128, 6, 8]); bc_d = din("bc_rows", [4, D]); fmv_d = din("fmvec", [128, 11, 8])
    wrg_d = din("wrgBD", [128, 32, 128]); cvd_d = din("convD", [128, 32, 128])
    rt_d = din("RT", [128, 16, 192]); ak_d = din("AK64", [64, 32, 128]); eq_d = din("EQ64", [64, S])
    idn_d = din("ident", [128, 128])
    out_d = nc.dram_tensor("out", [S, D], F32, kind="ExternalOutput").ap()
    dbg_d = {}
    if dbg:
        for nm, shp, dt_ in (("d_uT", [128, 8, NTP], BF16), ("d_hg", [128, 8, S], BF16), ("d_oT", [128, 8, S], BF16),
                             ("d_ym", [128, 8, S], BF16)):
            dbg_d[nm] = nc.dram_tensor(nm, shp, dt_, kind="ExternalOutput").ap()

    def wview(w, c0, n):
        return w[:, c0:c0 + n].rearrange("(kc p) n -> p kc n", p=128)

    ARENA_BYTES = 206 * 1024
    with ExitStack() as top0:
        sc.alloc_sems(top0)
        big = top0.enter_context(nc.sbuf_tensor("sb_big", [128, ARENA_BYTES // 2], BF16))
        arena = Arena(big, ARENA_BYTES)
        top = top0.enter_context(Stage(arena))

        def sb(st, name, shape, dt_):
            ap, h = arena.alloc(shape, dt_)
            st.handles.append(h)
            return ap

        ident = sb(top, "ident", [128, 128], BF16)
        ones = sb(top, "ones", [128, 128], BF16)
        fmv = sb(top, "fmv", [128, 11, 8], F32)
        hbv = sb(top, "hbv", [128, 4, 8], F32)
        hcv = sb(top, "hcv", [128, 2, 8], F32)
        dm = sb(top, "dm", [128, 6, 8], F32)
        Gb = sb(top, "Gb", [128, 2, D], F32)
        mh = sb(top, "mh", [128, 32], F32)
        mid = Stage(arena)
        uT = sb(mid, "uT", [128, 8, NTP], BF16)
        scb = sb(mid, "scb", [128, 8, 2], BF16); rep = sb(mid, "rep", [128, 8, 128], BF16)
        bada = sb(mid, "bada", [128, 6, 8], F32); mraw = sb(mid, "mraw", [128, 4, 8, 2], F32)
        fmidx = {0: 0, 1: 1, 3: 2, 4: 3}

        def ada_group(g, ps, w, wn, bcr, bcn, r_b, r_g):
            if g in fmidx:
                gi = fmidx[g]
                for oc in range(8):
                    for kc in range(8):
                        sc.op("pe", lambda: PE.matmul(ps[:, 0, (gi * 8 + oc) * 2:(gi * 8 + oc) * 2 + 2],
                                                      lhsT=w[:, kc, oc * 128:(oc + 1) * 128], rhs=scb[:, kc, :],
                                                      start=(kc == 0), stop=(kc == 7)),
                              reads=[wn, "scb"], writes=[("ps", 0)])
                pv = ps[:, 0, 0:64].rearrange("p (g o t) -> p g o t", g=4, o=8, t=2)
                for col in range(2):
                    sc.op("dve", lambda: DVE.tensor_tensor(out=mraw[:, gi, :, col], in0=pv[:, gi, :, col], in1=bada[:, g, :],
                                                           op=ALU.add), reads=[("ps", 0), "bada"], writes=["mraw"])
            else:
                bi = 0 if g == 2 else 1
                for half in range(2):
                    for kc in range(8):
                        sc.op("pe", lambda: PE.matmul(ps[:, 1 + half, :], lhsT=rep[:, kc, :], rhs=w[:, kc, half * 512:(half + 1) * 512],
                                                      start=(kc == 0), stop=(kc == 7)),
                              reads=[wn, "rep"], writes=[("ps", 1 + half)])
                    hs = slice(half * 512, (half + 1) * 512)
                    sc.op("dve", lambda: DVE.tensor_tensor(out=Gb[:, bi, hs], in0=ps[:, 1 + half, :], in1=bcr[:, r_b, hs],
                                                           op=ALU.add), reads=[("ps", 1 + half), bcn], writes=[("Gb", bi)])
                    sc.op("dve", lambda: DVE.tensor_tensor(out=Gb[:, bi, hs], in0=Gb[:, bi, hs], in1=bcr[:, r_g, hs],
                                                           op=ALU.mult), reads=[bcn], writes=[("Gb", bi)])

        with Stage(arena) as st:
            ps = st.enter_context(nc.psum_tensor("ps0", [128, 8, 512], F32))
            c2 = sb(st, "c2", [128, 8, 2], F32); sl = sb(st, "sl", [128, 8, 2], F32)
            wa = [sb(st, "wa%d" % i, [128, 8, D], BF16) for i in range(2)]
            xall = sb(st, "xall", [128, 6, D], F32)
            junk = sb(st, "junk", [128, D], BF16)
            ss = sb(st, "ss", [128, 18], F32); vv = sb(st, "vv", [128, 18], F32); rstd = sb(st, "rstd", [128, 18], F32)
            xnb = [sb(st, "xnb%d" % i, [128, D], BF16) for i in range(4)]
            lamt = sb(st, "lamt", [128, 2, 8], F32)

            sc.dma("sp", lambda: SP.dma_start(out=c2[:], in_=cc2_d), writes=["c2"])
            sc.dma("sp", lambda: SP.dma_start(out=fmv[:], in_=fmv_d), writes=["fmv"])
            sc.dma("sp", lambda: SP.dma_start(out=bada[:], in_=bada_d), writes=["bada"])
            sc.dma("pool", lambda: POOL.dma_start(out=ident[:], in_=idn_d), writes=["ident"])
            order = [0, 1, 2, 3, 4, 5]
            sc.dma("pool", lambda: POOL.dma_start(out=wa[0][:], in_=wview(wada_d, 0, D)), writes=["wa0"])
            sc.dma("pool", lambda: POOL.dma_start(out=wa[1][:], in_=wview(wada_d, D, D)), writes=["wa1"])

            sc.op("dve", lambda: DVE.memset(ones[:], 1.0), writes=["ones"])
            sc.op("dve", lambda: DVE.memset(mh[:], -0.5), writes=["mh"])
            sc.op("dve", lambda: DVE.memset(uT[:, :, 0:LAT0], 0.0), writes=["uTpad"])
            sc.op("dve", lambda: DVE.memset(uT[:, :, LAT0 + S:CTX0], 0.0), writes=["uTpad"])
            sc.op("dve", lambda: DVE.tensor_scalar(out=hbv[:], in0=fmv[:, 5:9, :], scalar1=0.5, scalar2=None, op0=ALU.mult),
                  reads=["fmv"], writes=["hbv"])
            sc.op("act", lambda: ACT.activation(out=lamt[:], in_=fmv[:, 9:11, :], func=AF.Exp, scale=-1.0), reads=["fmv"], writes=["lamt"])
            sc.op("act", lambda: ACT.activation(out=lamt[:], in_=lamt[:], func=AF.Ln, bias=1.0, scale=1.0), reads=["lamt"], writes=["lamt"])
            sc.op("dve", lambda: DVE.tensor_scalar(out=hcv[:], in0=lamt[:], scalar1=-4.0, scalar2=None, op0=ALU.mult),
                  reads=["lamt"], writes=["hcv"])
            sc.op("act", lambda: ACT.activation(out=sl[:], in_=c2[:], func=AF.Silu), reads=["c2"], writes=["sl"])
            sc.op("dve", lambda: DVE.tensor_copy(out=scb[:], in_=sl[:]), reads=["sl"], writes=["scb"])
            for kc in range(8):
                sc.op("dve", lambda kc=kc: DVE.tensor_scalar(out=rep[:, kc, :], in0=ones[:], scalar1=sl[:, kc, 0:1], scalar2=None,
                                                           op0=ALU.mult), reads=["sl", "ones"], writes=["rep"])

            ada_group(0, ps, wa[0], "wa0", None, None, 0, 0)
            ada_group(1, ps, wa[1], "wa1", None, None, 0, 0)
            sc.op("dve", lambda: DVE.scalar_tensor_tensor(out=dm[:, 0, :], in0=mraw[:, 1, :, 0], scalar=1.0, in1=fmv[:, 0, :],
                                                          op0=ALU.add, op1=ALU.mult), reads=["mraw", "fmv"], writes=["dm"])
            sc.op("dve", lambda: DVE.tensor_copy(out=dm[:, 1, :], in_=mraw[:, 0, :, 0]), reads=["mraw"], writes=["dm"])
            sc.op("dve", lambda: DVE.scalar_tensor_tensor(out=dm[:, 2, :], in0=mraw[:, 1, :, 1], scalar=1.0, in1=fmv[:, 0, :],
                                                          op0=ALU.add, op1=ALU.mult), reads=["mraw", "fmv"], writes=["dm"])
            sc.op("dve", lambda: DVE.tensor_copy(out=dm[:, 3, :], in_=mraw[:, 0, :, 1]), reads=["mraw"], writes=["dm"])

            def xload(tt):
                src = x_d[tt * 128:(tt + 1) * 128, :] if tt < 16 else ctx_d[(tt - 16) * 128:(tt - 15) * 128, :]
                sc.dma("sp", lambda: SP.dma_start(out=xall[:, tt % 6, :], in_=src), writes=[("xall", tt % 6)])

            for tt in range(5):
                xload(tt)
            groups = [(0, 4, LAT0), (4, 4, LAT0 + 512), (8, 4, LAT0 + 1024), (12, 4, LAT0 + 1536), (16, 2, CTX0)]
            tpb = [ps[:, 4 + h, :].bitcast(BF16).rearrange("p (k t) -> p k t", k=2) for h in range(4)]
            for gi_, (t0, nt, col0) in enumerate(groups):
                for j in range(nt):
                    tt = t0 + j
                    if tt + 5 < 18:
                        xload(tt + 5)
                    xs = tt % 6
                    slot = tt % 4
                    sc.op("act", lambda: ACT.activation(out=junk[:], in_=xall[:, xs, :], func=AF.Square, accum_out=ss[:, tt:tt + 1]),
                          reads=[("xall", xs)], writes=["junk", ("ss", tt)])
                    sc.op("dve", lambda: DVE.tensor_scalar(out=vv[:, tt:tt + 1], in0=ss[:, tt:tt + 1], scalar1=1.0 / D, scalar2=EPS, op0=ALU.mult, op1=ALU.add),
                          reads=[("ss", tt)], writes=[("vv", tt)])
                    sc.op("pool", lambda: POOL.tensor_tensor(out=rstd[:, tt:tt + 1], in0=vv[:, tt:tt + 1], in1=mh[:, 0:1], op=ALU.pow),
                          reads=[("vv", tt), "mh"], writes=[("rstd", tt)])
                    sc.op("act", lambda: ACT.activation(out=xnb[slot][:], in_=xall[:, xs, :], func=AF.Identity, scale=rstd[:, tt:tt + 1]),
                          reads=[("xall", xs), ("rstd", tt)], writes=[("xnb", slot)])
                    for kc in range(8):
                        sc.op("pe", lambda: PE.transpose(out=tpb[kc // 2][:, kc % 2, j * 128:(j + 1) * 128],
                                                         in_=xnb[slot][:, kc * 128:(kc + 1) * 128], identity=ident[:]),
                              reads=[("xnb", slot), "ident"], writes=[("ps", 4 + kc // 2)])
                si = 0 if t0 < 16 else 2
                for kc in range(8):
                    n = nt * 128
                    sc.op("dve", lambda: DVE.tensor_scalar(
                        out=uT[:, kc, col0:col0 + n], in0=tpb[kc // 2][:, kc % 2, 0:n], scalar1=dm[:, si, kc:kc + 1],
                        scalar2=dm[:, si + 1, kc:kc + 1], op0=ALU.mult, op1=ALU.add),
                        reads=[("ps", 4 + kc // 2), "dm"], writes=[("uT", kc)])
            if dbg:
                sc.dma("sp", lambda: SP.dma_start(out=dbg_d["d_uT"], in_=uT[:]), reads=[("uT", k) for k in range(8)] + ["uTpad"])
            sc.barrier()
        if stage_limit <= 0:
            sc.barrier(final=True)
            mid.close()
            return

        hgT = sb(mid, "hgT", [128, 8, S], BF16)
        pieces = [(512 * p, 512, LAT0 + 512 * p, 0, S) for p in range(4)] + [(S, C, CTX0, S, S + C)]
        with Stage(arena) as st:
            ps = st.enter_context(nc.psum_tensor("ps1", [128, 8, 512], F32))
            wxs = [sb(st, "wxs%d" % i, [128, 8, 128], BF16) for i in range(2)]; wgs = [sb(st, "wgs%d" % i, [128, 8, 128], BF16) for i in range(2)]
            wrgs = [sb(st, "wrgs%d" % i, [128, 4, 128], BF16) for i in range(2)]; cvds = [sb(st, "cvds%d" % i, [128, 4, 128], BF16) for i in range(2)]
            xrb = [sb(st, "xrb%d" % i, [128, NL], BF16) for i in range(2)]
            xcb = [sb(st, "xcb%d" % i, [128, NL], BF16) for i in range(2)]
            trb = [[sb(st, "tr%d_%d" % (i, d), [128, NL], F32) for d in range(2)] for i in range(2)]
            tib = [sb(st, "ti%d" % i, [128, NL], F32) for i in range(2)]
            avb = [sb(st, "av%d" % i, [128, NL], F32) for i in range(2)]
            a2b = [sb(st, "a2%d" % i, [128, NL], F32) for i in range(2)]
            wrg4 = wrg_d.rearrange("p (a c) m -> p a c m", c=8)
            cvd4 = cvd_d.rearrange("p (a c) m -> p a c m", c=8)
            bk = [0, 0]

            def nbz(z):
                b = 4 * z + bk[z] % 4
                bk[z] += 1
                return b

            def front(cc):
                z = cc % 2
                yield ("pool", lambda: POOL.dma_start(out=wxs[z][:], in_=wview(win_d, cc * 128, 128)), [], [("wxs", z)], 1500.0, "dma")
                yield ("pool", lambda: POOL.dma_start(out=cvds[z][:], in_=cvd4[:, :, cc, :]), [], [("cvds", z)], 500.0, "dma")
                yield ("pool", lambda: POOL.dma_start(out=wrgs[z][:], in_=wrg4[:, :, cc, :]), [], [("wrgs", z)], 500.0, "dma")
                yield ("pool", lambda: POOL.dma_start(out=wgs[z][:], in_=wview(win_d, D + cc * 128, 128)), [], [("wgs", z)], 1500.0, "dma")
                for (l0, n, u0, lo, hi) in pieces:
                    b = nbz(z)
                    for kc in range(8):
                        yield ("pe", lambda: PE.matmul(ps[:, b, 0:n], lhsT=wxs[z][:, kc, :], rhs=uT[:, kc, u0:u0 + n], start=(kc == 0), stop=(kc == 7)),
                               [("wxs", z), ("uT", kc)], [("ps", b)], c_pe(n))
                    yield ("dve", lambda: DVE.tensor_copy(out=xrb[z][:, l0:l0 + n], in_=ps[:, b, 0:n]), [("ps", b)], [("xrb", z, l0)], c_dve(n))
                for (l0, n, u0, lo, hi) in pieces:
                    b = nbz(z)
                    for ti_, tap in enumerate([2, 0, 1, 3]):
                        o = tap - 2
                        ta = max(l0, lo - o); tb = min(l0 + n, hi - o)
                        yield ("pe", lambda: PE.matmul(ps[:, b, ta - l0:tb - l0], lhsT=cvds[z][:, tap, :], rhs=xrb[z][:, ta + o:tb + o],
                                                       start=(ti_ == 0), stop=(ti_ == 3)),
                               [("cvds", z)] + [("xrb", z, q_[0]) for q_ in pieces], [("ps", b)], c_pe(n))
                    yield ("dve", lambda: DVE.tensor_scalar(out=xcb[z][:, l0:l0 + n], in0=ps[:, b, 0:n], scalar1=fmv[:, 2, cc:cc + 1], scalar2=None,
                                                            op0=ALU.add), [("ps", b), "fmv"], [("xcb", z, l0)], c_dve(n))

            def back(cc):
                z = cc % 2
                ti, av, a2 = tib[z], avb[z], a2b[z]
                ge = xrb[z]
                xcb_all = [("xcb", z, q_[0]) for q_ in pieces]
                xrb_all = [("xrb", z, q_[0]) for q_ in pieces]
                for d in range(2):
                    tr = trb[z][d]
                    for (l0, n, u0, lo, hi) in pieces:
                        for g_ in range(2):
                            b = nbz(z)
                            yield ("pe", lambda: PE.matmul(ps[:, b, 0:n], lhsT=wrgs[z][:, d * 2 + g_, :], rhs=xcb[z][:, l0:l0 + n], start=True, stop=True),
                                   [("wrgs", z), ("xcb", z, l0)], [("ps", b)], c_pe(n))
                            dst = tr if g_ == 0 else ti
                            yield ("act", lambda: ACT.activation(out=dst[:, l0:l0 + n], in_=ps[:, b, 0:n], func=AF.Tanh,
                                                                 bias=hbv[:, d * 2 + g_, cc:cc + 1], scale=0.5),
                                   [("ps", b), "hbv"], [("tr", z, d), ("hs", z, d)] if g_ == 0 else [("ti", z)] + [("bb", z, q_) for q_ in range(3)], c_act(n))
                    rngs = [(S, NL), (0, 1024), (1024, S)] if d == 0 else [(S, NL), (1024, S), (0, 1024)]
                    for ri, (r0, r1) in enumerate(rngs):
                        yield ("act", lambda: ACT.activation(out=av[:, r0:r1], in_=tr[:, r0:r1], func=AF.Exp, bias=hcv[:, d, cc:cc + 1],
                                                             scale=hcv[:, d, cc:cc + 1]),
                               [("tr", z, d), "hcv"], [("av", z, ri)], c_act(r1 - r0) + (1300.0 if ri == 0 else 0.0))
                    for ri, (r0, r1) in enumerate(rngs):
                        yield ("pool", lambda: POOL.tensor_tensor(out=a2[:, r0:r1], in0=av[:, r0:r1], in1=av[:, r0:r1], op=ALU.mult),
                               [("av", z, ri)], [("a2", z, ri)], c_pool(r1 - r0))
                    for ri, (r0, r1) in enumerate(rngs):
                        yield ("act", lambda: ACT.activation(out=a2[:, r0:r1], in_=a2[:, r0:r1], func=AF.Sqrt, bias=1.0, scale=-1.0),
                               [("a2", z, ri)], [("a2", z, ri)], c_act(r1 - r0) + (1300.0 if ri == 0 else 0.0))
                    for ri, (r0, r1) in enumerate(rngs):
                        yield ("pool", lambda: POOL.tensor_tensor(out=a2[:, r0:r1], in0=a2[:, r0:r1], in1=xcb[z][:, r0:r1], op=ALU.mult),
                               [("a2", z, ri)] + xcb_all, [("a2", z, ri)], c_pool(r1 - r0))
                    for ri, (r0, r1) in enumerate(rngs):
                        yield ("dve", lambda: DVE.scalar_tensor_tensor(out=ti[:, r0:r1], in0=ti[:, r0:r1], scalar=1.0, in1=a2[:, r0:r1],
                                                                       op0=ALU.add, op1=ALU.mult),
                               [("ti", z), ("a2", z, ri)], [("bb", z, ri)], c_dve(r1 - r0, True))
                        rd = [("av", z, ri), ("bb", z, ri), ("hs", z, d)]
                        wr_ = [("hs", z, d)]
                        if ri == 0:
                            init = 0.0
                        elif d == 0:
                            init = tr[:, rngs[ri - 1][1] - 1:rngs[ri - 1][1]]
                        else:
                            init = tr[:, rngs[ri - 1][0]:rngs[ri - 1][0] + 1]
                        if d == 0:
                            yield ("dve", lambda: DVE.tensor_tensor_scan(out=tr[:, r0:r1], data0=av[:, r0:r1], data1=ti[:, r0:r1], initial=init,
                                                                         op0=ALU.mult, op1=ALU.add), rd, wr_, c_dve(r1 - r0, True))
                        else:
                            yield ("dve", lambda: DVE.tensor_tensor_scan(out=tr[:, r0:r1][:, ::-1], data0=av[:, r0:r1][:, ::-1],
                                                                         data1=ti[:, r0:r1][:, ::-1], initial=init,
                                                                         op0=ALU.mult, op1=ALU.add), rd, wr_, c_dve(r1 - r0, True))
                for p in range(4):
                    b = nbz(z)
                    for kc in range(8):
                        yield ("pe", lambda: PE.matmul(ps[:, b, :], lhsT=wgs[z][:, kc, :], rhs=uT[:, kc, LAT0 + 512 * p:LAT0 + 512 * (p + 1)],
                                                       start=(kc == 0), stop=(kc == 7)), [("wgs", z), ("uT", kc)], [("ps", b)], c_pe(512))
                    yield ("act", lambda: ACT.activation(out=ge[:, 512 * p:512 * (p + 1)], in_=ps[:, b, :], func=AF.Gelu_apprx_tanh),
                           [("ps", b)], [("xrb", z, 512 * p)], c_act(512) + (1300.0 if p == 0 else 0.0))
                yield ("dve", lambda: DVE.tensor_tensor(out=trb[z][1][:, 0:S], in0=trb[z][0][:, 0:S], in1=trb[z][1][:, 0:S], op=ALU.add),
                       [("hs", z, 0), ("hs", z, 1)], [("hs", z, 1)], c_dve(S, True))
                yield ("dve", lambda: DVE.scalar_tensor_tensor(out=hgT[:, cc, :], in0=trb[z][1][:, 0:S], scalar=0.5, in1=ge[:, 0:S], op0=ALU.mult,
                                                               op1=ALU.mult), [("hs", z, 1)] + xrb_all, [("hgT", cc)], c_dve(S, True))

            def stream(z):
                for cc in range(z, 8, 2):
                    yield from front(cc)
                    yield from back(cc)

            sc.run_list([stream(0), stream(1)])
            if dbg:
                sc.dma("sp", lambda: SP.dma_start(out=dbg_d["d_hg"], in_=hgT[:]), reads=[("hgT", k) for k in range(8)])
            sc.barrier()
        if stage_limit <= 1:
            sc.barrier(final=True)
            mid.close()
            return

        oT = sb(mid, "oT", [128, 8, S], BF16)
        with Stage(arena) as st:
            ps = st.enter_context(nc.psum_tensor("ps2", [128, 8, 512], F32))
            RT = sb(st, "RT", [128, 16, 192], F32)
            wq = [sb(st, "wq%d" % i, [128, 8, 128], BF16) for i in range(2)]
            wk = [sb(st, "wk%d" % i, [128, 8, 128], BF16) for i in range(2)]
            wv = [sb(st, "wv%d" % i, [128, 8, 128], BF16) for i in range(2)]
            qTA = sb(st, "qTA", [128, S], BF16); qTB = sb(st, "qTB", [128, S], BF16)
            kT = [sb(st, "kT0", [128, NTP], BF16)] * 2
            kTbA = sb(st, "kTbA", [128, 32, 128], BF16); kTbB = sb(st, "kTbB", [128, 32, 128], BF16)
            vT = sb(st, "vT", [128, 34 * 128], BF16)
            VV = sb(st, "VV", [128, 34, 192], BF16)
            pt = [sb(st, "pt%d" % i, [128, 512], BF16) for i in range(4)]
            sbf = [sb(st, "sbf%d" % i, [128, 192], F32) for i in range(4)]
            rec = [sb(st, "rec%d" % i, [128, 512], F32) for i in range(4)]
            sring = [0]
            sc.dma("sp", lambda: SP.dma_start(out=RT[:], in_=rt_d), writes=["RT"])
            sc.dma("pool", lambda: POOL.dma_start(out=qTA[64:128, :], in_=eq_d), writes=["qTAc"])
            sc.dma("pool", lambda: POOL.dma_start(out=qTB[0:64, :], in_=eq_d), writes=["qTBc"])
            sc.dma("pool", lambda: POOL.dma_start(out=kTbA[64:128, :, :], in_=ak_d), writes=["kTbAc"])
            sc.dma("pool", lambda: POOL.dma_start(out=kTbB[0:64, :, :], in_=ak_d), writes=["kTbBc"])
            for i in range(1):
                sc.op("dve", lambda i=i: DVE.memset(kT[i][:, 0:LAT0], 0.0), writes=[("kT", i)])
                sc.op("dve", lambda i=i: DVE.memset(kT[i][:, LAT0 + S:CTX0], 0.0), writes=[("kT", i)])
                sc.op("dve", lambda i=i: DVE.memset(VV[:, :, 64:128], 1.0), writes=[("VV", 0)])

            def load_w(i):
                s_ = i % 2
                sc.dma("pool", lambda: POOL.dma_start(out=wq[s_][:], in_=wview(win_d, 2 * D + i * 128, 128)), writes=[("wq", s_)])
                sc.dma("pool", lambda: POOL.dma_start(out=wk[s_][:], in_=wview(win_d, 3 * D + i * 128, 128)), writes=[("wk", s_)])
                sc.dma("pool", lambda: POOL.dma_start(out=wv[s_][:], in_=wview(win_d, 4 * D + i * 128, 128)), writes=[("wv", s_)])

            load_w(0)
            mmb = [0]

            def nmm():
                b = mmb[0]
                mmb[0] = (b + 1) % 4
                return b

            lcb = [0]
            r3 = [0]
            for i in range(DBG["npairs"]):
                s_ = i % 2
                if i + 1 < 8:
                    load_w(i + 1)
                for p in range(4):
                    b = nmm()
                    for kc in range(8):
                        sc.op("pe", lambda b=b, kc=kc, p=p: PE.matmul(ps[:, b, :], lhsT=wq[s_][:, kc, :], rhs=uT[:, kc, LAT0 + 512 * p:LAT0 + 512 * (p + 1)],
                                                                     start=(kc == 0), stop=(kc == 7)), reads=[("wq", s_), ("uT", kc)], writes=[("ps", b)])
                    sc.op("act", lambda b=b, p=p: ACT.activation(out=qTA[0:64, 512 * p:512 * (p + 1)], in_=ps[0:64, b, :], func=AF.Identity, scale=0.125),
                          reads=[("ps", b)], writes=[("qT", 0)])
                    sc.op("act", lambda b=b, p=p: ACT.activation(out=qTB[64:128, 512 * p:512 * (p + 1)], in_=ps[64:128, b, :], func=AF.Identity, scale=0.125),
                          reads=[("ps", b)], writes=[("qT", 1)])
                for (l0, n, u0, lo, hi) in pieces:
                    b = nmm()
                    for kc in range(8):
                        sc.op("pe", lambda b=b, kc=kc, n=n, u0=u0: PE.matmul(ps[:, b, 0:n], lhsT=wk[s_][:, kc, :], rhs=uT[:, kc, u0:u0 + n],
                                                                            start=(kc == 0), stop=(kc == 7)), reads=[("wk", s_), ("uT", kc)], writes=[("ps", b)])
                    sc.op("dve", lambda b=b, n=n, u0=u0: DVE.tensor_copy(out=kT[s_][:, u0:u0 + n], in_=ps[:, b, 0:n]),
                          reads=[("ps", b)], writes=[("kT", 0)])
                for j in range(4 if DBG["kcopy"] else 0):
                    base = LAT0 + 16 * j - 8
                    for (kt_, hp_, nm_) in ((kTbA, slice(0, 64), 0), (kTbB, slice(64, 128), 1)):
                        sc.op("pool", lambda: POOL.tensor_copy(
                            out=kt_[hp_, j * 8:(j + 1) * 8, :].rearrange("p g (r c) -> p g r c", r=4, c=32),
                            in_=kT[s_][hp_, base:base + 2048].rearrange("p (g r c) -> p g r c", g=8, r=4, c=64)[:, :, :, 0:32]),
                            reads=[("kT", 0)], writes=[("kTb", nm_)])
                for b4 in range(9 if DBG["vproj"] else 0):
                    b = nmm()
                    blks = list(range(b4 * 4, min(b4 * 4 + 4, 34)))
                    for bi_, blk in enumerate(blks):
                        for kc in range(8):
                            if blk < 32:
                                j, g = blk // 8, blk % 8
                                base = LAT0 + 4 * g * 64 + 16 * j - 8
                                mv = uT[:, kc, base:base + 256].rearrange("p (r c) -> p r c", c=64)[:, :, 0:32]
                            else:
                                t = blk - 32
                                mv = uT[:, kc, CTX0 + 128 * t:CTX0 + 128 * (t + 1)]
                            sc.op("pe", lambda: PE.matmul(ps[:, b, bi_ * 128:(bi_ + 1) * 128], lhsT=wv[s_][:, kc, :], rhs=mv,
                                                          start=(kc == 0), stop=(kc == 7)),
                                  reads=[("wv", s_), ("uT", kc)], writes=[("ps", b)])
                    nb_ = len(blks)
                    sc.op("act", lambda: ACT.activation(out=vT[:, blks[0] * 128:(blks[0] + nb_) * 128], in_=ps[:, b, 0:nb_ * 128], func=AF.Identity),
                          reads=[("ps", b)], writes=[("vT", b4)])
                for b8 in range(5 if DBG["vtr"] else 0):
                    b = 4 + b8 % 2
                    blks = list(range(b8 * 8, min(b8 * 8 + 8, 34)))
                    tpv = ps[:, b, :].bitcast(BF16).rearrange("p (k d) -> p k d", d=128)
                    for bi_, blk in enumerate(blks):
                        sc.op("pe", lambda: PE.transpose(out=tpv[:, bi_, :], in_=vT[:, blk * 128:(blk + 1) * 128], identity=ident[:]),
                              reads=[("vT", blk // 4), "ident"], writes=[("ps", b)])
                    nb_ = len(blks)
                    sc.op("dve", lambda: DVE.tensor_copy(
                        out=VV[:, blks[0]:blks[0] + nb_, :].rearrange("p k (a c) -> p k a c", c=64)[:, :, 0::2, :],
                        in_=tpv[:, 0:nb_, :].rearrange("p k (a c) -> p k a c", c=64)), reads=[("ps", b)], writes=[("VV", 0)])
                def unit(j, hh, kind, idx):
                    hp = slice(0, 64) if hh == 0 else slice(64, 128)
                    Vx = VV[:, :, 0:128] if hh == 0 else VV[:, :, 64:192]
                    h = 2 * i + hh
                    ab = 4 + (j % 2) * 2 + hh
                    qx = qTA if hh == 0 else qTB
                    q3 = qx[hp, :].rearrange("p (r c) -> p r c", c=64)
                    q3f = qx[:, :].rearrange("p (r c) -> p r c", c=64)
                    kx = kTbA if hh == 0 else kTbB
                    b = sring[0] % 4
                    sring[0] += 1
                    if kind == "c":
                        nq, c0, vblk = 512, 0, 32 + idx
                        sc.op("pe", lambda: PE.matmul(ps[:, b, :], lhsT=kT[s_][hp, CTX0 + 128 * idx:CTX0 + 128 * (idx + 1)],
                                                      rhs=q3[:, :, 16 * j:16 * j + 16], start=True, stop=True),
                              reads=[("kT", 0), ("qT", hh)], writes=[("ps", b)])
                    else:
                        g = idx
                        lo_ = max(0, 4 - 4 * g); hi_ = min(12, 36 - 4 * g)
                        nq = (hi_ - lo_) * 16
                        qr0 = 4 * g - 4 + lo_
                        c0, vblk = qr0 * 16, j * 8 + g
                        sc.op("pe", lambda: PE.matmul(ps[:, b, 0:nq], lhsT=kx[:, j * 8 + g, :],
                                                      rhs=q3f[:, qr0:qr0 + (hi_ - lo_), 16 * j:16 * j + 16], start=True, stop=True),
                              reads=[("kTb", hh), ("qT", hh), "qTAc", "qTBc", "kTbAc", "kTbBc"], writes=[("ps", b)])
                    yield
                    yield
                    r_ = r3[0] % 4
                    r3[0] += 1
                    if kind == "c":
                        sc.op("act", lambda: ACT.activation(out=pt[r_][:], in_=ps[:, b, :], func=AF.Exp),
                              reads=[("ps", b)], writes=[("pt", r_)])
                    else:
                        sc.op("dve", lambda: DVE.tensor_tensor(out=sbf[r_][:, 0:nq], in0=ps[:, b, 0:nq], in1=RT[:, h, lo_ * 16:hi_ * 16], op=ALU.add),
                              reads=[("ps", b), "RT"], writes=[("sbf", r_)])
                        yield
                        sc.op("act", lambda: ACT.activation(out=pt[r_][:, 0:nq], in_=sbf[r_][:, 0:nq], func=AF.Exp),
                              reads=[("sbf", r_)], writes=[("pt", r_)])
                    yield
                    yield
                    first = (kind == "c" and idx == 0)
                    last = (kind == "l" and idx == 7)
                    sc.op("pe", lambda: PE.matmul(ps[:, ab, c0:c0 + nq], lhsT=Vx[:, vblk, :], rhs=pt[r_][:, 0:nq], start=first, stop=last),
                          reads=[("VV", 0), ("pt", r_)], writes=[("ps", ab)])
                    if last:
                        yield
                        op_ = slice(0, 64) if hh == 0 else slice(64, 128)
                        dp_ = slice(64, 128) if hh == 0 else slice(0, 64)
                        rc = rec[(j % 2) * 2 + hh]
                        for q_ in range(4):
                            cs_ = slice(128 * q_, 128 * (q_ + 1))
                            sc.op("dve", lambda: DVE.reciprocal(out=rc[op_, cs_], in_=ps[dp_, ab, cs_]),
                                  reads=[("ps", ab)], writes=[("rec", (j % 2) * 2 + hh, q_)])
                            yield
                            sc.op("dve", lambda: DVE.tensor_tensor(
                                out=oT[op_, i, :].rearrange("p (r c) -> p r c", c=64)[:, 8 * q_:8 * q_ + 8, 16 * j:16 * j + 16],
                                in0=ps[op_, ab, cs_].rearrange("p (r c) -> p r c", c=16), in1=rc[op_, cs_].rearrange("p (r c) -> p r c", c=16),
                                op=ALU.mult), reads=[("ps", ab), ("rec", (j % 2) * 2 + hh, q_)], writes=[("oT", i)])
                            yield

                units = []
                for j in range(DBG["nj"] if DBG["att"] else 0):
                    for kind, idx in [("c", 0), ("c", 1)] + [("l", g) for g in range(8)]:
                        units += [unit(j, 0, kind, idx), unit(j, 1, kind, idx)]
                run_skewed(units)
            if dbg:
                sc.dma("sp", lambda: SP.dma_start(out=dbg_d["d_oT"], in_=oT[:]), reads=[("oT", k) for k in range(8)])
            sc.barrier()
        if stage_limit <= 2:
            sc.barrier(final=True)
            mid.close()
            return

        ymT = sb(top, "ymT", [128, 8, S], BF16)
        with Stage(arena) as st:
            ps = st.enter_context(nc.psum_tensor("ps3", [128, 8, 512], F32))
            wsl = [[sb(st, "wsl%d_%d" % (i, k), [128, 8, 128], BF16) for k in range(4)] for i in range(2)]
            gA = [sb(st, "gA%d" % i, [128, 512], F32) for i in range(2)]
            gB = [sb(st, "gB%d" % i, [128, 512], F32) for i in range(2)]
            t1 = [sb(st, "t1%d" % i, [128, 512], F32) for i in range(2)]
            t2 = [sb(st, "t2%d" % i, [128, 512], F32) for i in range(2)]

            wa3 = sb(st, "wa3", [128, 8, D], BF16)
            bcr2 = sb(st, "bcr2", [128, 2, D], F32)

            def ada_dma(g):
                sc.dma("pool", lambda: POOL.dma_start(out=wa3[:], in_=wview(wada_d, g * D, D)), writes=["wa3"])
                if g in (2, 5):
                    bi = 0 if g == 2 else 1
                    sc.dma("sp", lambda: SP.dma_start(out=bcr2[:, 0, :], in_=bc_d[bi:bi + 1, :].partition_broadcast(128)[:, 0, :]), writes=["bcr2"])
                    sc.dma("sp", lambda: SP.dma_start(out=bcr2[:, 1, :], in_=bc_d[2 + bi:3 + bi, :].partition_broadcast(128)[:, 0, :]), writes=["bcr2"])

            ada_order = [2, 3, 4, 5]
            ada_dma(2)

            def load3(cc):
                s_ = cc % 2
                srcs = [wview(win_d, 5 * D + cc * 128, 128), wview(win_d, 6 * D + cc * 128, 128), wview(wlo_d, cc * 128, 128),
                        wview(wno_d, cc * 128, 128)]
                for k in range(4):
                    sc.dma("pool", lambda k=k: POOL.dma_start(out=wsl[s_][k][:], in_=srcs[k]), writes=[("wsl", s_, k)])

            load3(0)
            u_ = 0
            for cc in range(8):
                s_ = cc % 2
                if cc + 1 < 8:
                    load3(cc + 1)
                for p in range(4):
                    z = u_ % 2
                    u_ += 1
                    ts = slice(512 * p, 512 * (p + 1))
                    us = slice(LAT0 + 512 * p, LAT0 + 512 * (p + 1))
                    srcs = [(uT, us, "uT"), (uT, us, "uT"), (hgT, ts, "hgT"), (oT, ts, "oT")]
                    for k in range(4):
                        b = z * 4 + k
                        src, sl_, nm = srcs[k]
                        for kc in range(8):
                            sc.op("pe", lambda b=b, k=k, kc=kc, src=src, sl_=sl_: PE.matmul(ps[:, b, :], lhsT=wsl[s_][k][:, kc, :], rhs=src[:, kc, sl_],
                                                                                            start=(kc == 0), stop=(kc == 7)),
                                  reads=[("wsl", s_, k), (nm, kc)], writes=[("ps", b)])
                    sc.op("act", lambda z=z: ACT.activation(out=gA[z][:], in_=ps[:, z * 4 + 0, :], func=AF.Sigmoid, bias=fmv[:, 3, cc:cc + 1], scale=1.0),
                          reads=[("ps", z * 4 + 0), "fmv"], writes=[("gA", z)])
                    sc.op("act", lambda z=z: ACT.activation(out=gB[z][:], in_=ps[:, z * 4 + 1, :], func=AF.Sigmoid, bias=fmv[:, 4, cc:cc + 1], scale=1.0),
                          reads=[("ps", z * 4 + 1), "fmv"], writes=[("gB", z)])
                    sc.op("dve", lambda z=z: DVE.tensor_tensor(out=t1[z][:], in0=ps[:, z * 4 + 2, :], in1=gA[z][:], op=ALU.mult),
                          reads=[("ps", z * 4 + 2), ("gA", z)], writes=[("t1", z)])
                    sc.op("dve", lambda z=z: DVE.tensor_tensor(out=t2[z][:], in0=ps[:, z * 4 + 3, :], in1=gB[z][:], op=ALU.mult),
                          reads=[("ps", z * 4 + 3), ("gB", z)], writes=[("t2", z)])
                    sc.op("pool", lambda z=z, ts=ts: POOL.tensor_tensor(out=ymT[:, cc, ts], in0=t1[z][:], in1=t2[z][:], op=ALU.add),
                          reads=[("t1", z), ("t2", z)], writes=[("ymT", cc)])
                if cc % 2 == 1:
                    g_ = ada_order[cc // 2]
                    ada_group(g_, ps, wa3, "wa3", bcr2, "bcr2", 0, 1)
                    if cc // 2 + 1 < 4:
                        ada_dma(ada_order[cc // 2 + 1])
            sc.op("dve", lambda: DVE.scalar_tensor_tensor(out=dm[:, 4, :], in0=mraw[:, 3, :, 0], scalar=1.0, in1=fmv[:, 1, :],
                                                          op0=ALU.add, op1=ALU.mult), reads=["mraw", "fmv"], writes=["dm"])
            sc.op("dve", lambda: DVE.tensor_copy(out=dm[:, 5, :], in_=mraw[:, 2, :, 0]), reads=["mraw"], writes=["dm"])
            if dbg:
                sc.dma("sp", lambda: SP.dma_start(out=dbg_d["d_ym"], in_=ymT[:]), reads=[("ymT", k) for k in range(8)])
            sc.barrier()
        if stage_limit <= 3:
            sc.barrier(final=True)
            mid.close()
            return

        mid.close()
        with Stage(arena) as st:
            ps = st.enter_context(nc.psum_tensor("ps4", [128, 8, 512], F32))
            wo = sb(st, "wo", [128, 8, D], BF16)
            wr = [sb(st, "wr%d" % i, [128, 8, D], BF16) for i in range(3)]
            xres = sb(st, "xres", [128, 4, D], F32); x1c = sb(st, "x1c", [128, 4, D], F32)
            u2T = sb(st, "u2T", [128, 8, 512], BF16); h1T = sb(st, "h1T", [128, 32, 512], BF16)
            ost = [sb(st, "ost%d" % i, [128, D], F32) for i in range(4)]
            xnb = [sb(st, "xnq%d" % i, [128, D], BF16) for i in range(4)]
            rl = [sb(st, "rl0", [128, 512], F32)] * 2
            ssy = sb(st, "ssy", [128, 8], F32); vy = sb(st, "vy", [128, 4], F32); ry = sb(st, "ry", [128, 4], F32)
            sc.dma("pool", lambda: POOL.dma_start(out=wo[:], in_=wview(wo_d, 0, D)), writes=["wo"])
            loads = []
            for ck_ in range(4):
                loads += [wview(w1_d, g1 * D, D) for g1 in range(4)]
                loads += [w2_d[g2 * D:(g2 + 1) * D, :].rearrange("(kc p) n -> p kc n", p=128) for g2 in range(4)]
            wst = {"issued": 0, "used": 0}

            def issue_w():
                n = wst["issued"]
                if n < len(loads):
                    wst["issued"] += 1
                    sc.dma("pool", lambda: POOL.dma_start(out=wr[n % 3][:], in_=loads[n]), writes=[("wr", n % 3)])

            def next_w():
                n = wst["used"]
                wst["used"] += 1
                return n % 3

            for _ in range(3):
                issue_w()
            tpv = [ps[:, 6 + h, :].bitcast(BF16).rearrange("p (k t) -> p k t", k=8) for h in range(2)]
            def xres_load(ck):
                for tt in range(4):
                    r0 = (ck * 4 + tt) * 128
                    sc.dma("sp", lambda tt=tt, r0=r0: SP.dma_start(out=xres[:, tt, :], in_=x_d[r0:r0 + 128, :]), writes=[("xres", tt)])

            xres_load(0)
            for ck in range(4):
                def pro(ck, tt):
                    tok = slice((ck * 4 + tt) * 128, (ck * 4 + tt + 1) * 128)
                    yb = [2 * tt, 2 * tt + 1]
                    for half in range(2):
                        b = yb[half]
                        for kc in range(8):
                            sc.op("pe", lambda: PE.matmul(ps[:, b, :], lhsT=ymT[:, kc, tok], rhs=wo[:, kc, half * 512:(half + 1) * 512],
                                                          start=(kc == 0), stop=(kc == 7)), reads=["wo", ("ymT", kc)], writes=[("ps", b)])
                    yield
                    for half in range(2):
                        b = yb[half]
                        sc.op("act", lambda: ACT.activation(out=xnb[tt][:, 0:512], in_=ps[:, b, :], func=AF.Square,
                                                            accum_out=ssy[:, tt * 2 + half:tt * 2 + half + 1]),
                              reads=[("ps", b)], writes=[("xnq", tt), ("ssy", tt)])
                    yield
                    sc.op("dve", lambda: DVE.tensor_tensor(out=vy[:, tt:tt + 1], in0=ssy[:, 2 * tt:2 * tt + 1], in1=ssy[:, 2 * tt + 1:2 * tt + 2], op=ALU.add),
                          reads=[("ssy", tt)], writes=[("vy", tt)])
                    sc.op("dve", lambda: DVE.tensor_scalar(out=vy[:, tt:tt + 1], in0=vy[:, tt:tt + 1], scalar1=1.0 / D, scalar2=EPS, op0=ALU.mult, op1=ALU.add),
                          reads=[("vy", tt)], writes=[("vy", tt)])
                    yield
                    sc.op("pool", lambda: POOL.tensor_tensor(out=ry[:, tt:tt + 1], in0=vy[:, tt:tt + 1], in1=mh[:, 0:1], op=ALU.pow),
                          reads=[("vy", tt), "mh"], writes=[("ry", tt)])
                    yield
                    for half in range(2):
                        b = yb[half]
                        hs = slice(half * 512, (half + 1) * 512)
                        sc.op("dve", lambda: DVE.scalar_tensor_tensor(out=x1c[:, tt, hs], in0=ps[:, b, :], scalar=ry[:, tt:tt + 1], in1=Gb[:, 0, hs],
                                                                      op0=ALU.mult, op1=ALU.mult),
                              reads=[("ps", b), ("ry", tt), ("Gb", 0)], writes=[("x1c", tt)])
                    yield
                    sc.op("pool", lambda: POOL.tensor_tensor(out=x1c[:, tt, :], in0=x1c[:, tt, :], in1=xres[:, tt, :], op=ALU.add),
                          reads=[("x1c", tt), ("xres", tt)], writes=[("x1c", tt)])
                    yield
                    sc.op("act", lambda: ACT.activation(out=xnb[tt][:], in_=x1c[:, tt, :], func=AF.Square, accum_out=ssy[:, tt * 2:tt * 2 + 1]),
                          reads=[("x1c", tt)], writes=[("xnq", tt), ("ssy", tt)])
                    yield
                    sc.op("dve", lambda: DVE.tensor_scalar(out=vy[:, tt:tt + 1], in0=ssy[:, 2 * tt:2 * tt + 1], scalar1=1.0 / D, scalar2=EPS, op0=ALU.mult, op1=ALU.add),
                          reads=[("ssy", tt)], writes=[("vy", tt)])
                    yield
                    sc.op("pool", lambda: POOL.tensor_tensor(out=ry[:, tt:tt + 1], in0=vy[:, tt:tt + 1], in1=mh[:, 0:1], op=ALU.pow),
                          reads=[("vy", tt), "mh"], writes=[("ry", tt)])
                    yield
                    sc.op("act", lambda: ACT.activation(out=xnb[tt][:], in_=x1c[:, tt, :], func=AF.Identity, scale=ry[:, tt:tt + 1]),
                          reads=[("x1c", tt), ("ry", tt)], writes=[("xnq", tt)])
                    yield
                    tvw = ps[:, 2 * tt, :].bitcast(BF16).rearrange("p (k t) -> p k t", k=8)
                    for kc in range(8):
                        sc.op("pe", lambda: PE.transpose(out=tvw[:, kc, :], in_=xnb[tt][:, kc * 128:(kc + 1) * 128], identity=ident[:]),
                              reads=[("xnq", tt), "ident"], writes=[("ps", 2 * tt)])
                    yield
                    for kc in range(8):
                        sc.op("dve", lambda: DVE.tensor_scalar(out=u2T[:, kc, tt * 128:(tt + 1) * 128], in0=tvw[:, kc, :],
                                                               scalar1=dm[:, 4, kc:kc + 1], scalar2=dm[:, 5, kc:kc + 1], op0=ALU.mult, op1=ALU.add),
                              reads=[("ps", 2 * tt), "dm"], writes=[("u2T", tt)])

                if ck == 0:
                    run_skewed([pro(0, tt) for tt in range(4)])
                if ck + 1 < 4:
                    xres_load(ck + 1)
                for g1 in range(4):
                    s_ = next_w()
                    for f8 in range(8):
                        fc = g1 * 8 + f8
                        b = fc % 4
                        for kc in range(8):
                            sc.op("pe", lambda b=b, kc=kc, f8=f8, s_=s_: PE.matmul(ps[:, b, :], lhsT=wr[s_][:, kc, f8 * 128:(f8 + 1) * 128], rhs=u2T[:, kc, :],
                                                                                  start=(kc == 0), stop=(kc == 7)), reads=[("wr", s_)] + [("u2T", t_) for t_ in range(4)], writes=[("ps", b)])
                        z = fc % 2
                        sc.op("act", lambda b=b, z=z: ACT.activation(out=rl[z][:], in_=ps[:, b, :], func=AF.Relu), reads=[("ps", b)], writes=[("rl", 0)])
                        sc.op("dve", lambda b=b, z=z, fc=fc: DVE.tensor_tensor(out=h1T[:, fc, :], in0=ps[:, b, :], in1=rl[z][:], op=ALU.mult),
                              reads=[("ps", b), ("rl", 0)], writes=[("h1T", fc)])
                    issue_w()
                for g2 in range(4):
                    s_ = next_w()
                    for tt in range(4):
                        for half in range(2):
                            b = tt * 2 + half
                            for f8 in range(8):
                                fc = g2 * 8 + f8
                                sc.op("pe", lambda b=b, fc=fc, f8=f8, tt=tt, half=half, s_=s_: PE.matmul(
                                    ps[:, b, :], lhsT=h1T[:, fc, tt * 128:(tt + 1) * 128], rhs=wr[s_][:, f8, half * 512:(half + 1) * 512],
                                    start=(fc == 0), stop=(fc == 31)), reads=[("wr", s_), ("h1T", fc)], writes=[("ps", b)])
                    issue_w()
                def epi(ck, tt):
                    for half in range(2):
                        b = tt * 2 + half
                        sc.op("act", lambda: ACT.activation(out=xnb[tt][:, 0:512], in_=ps[:, b, :], func=AF.Square,
                                                            accum_out=ssy[:, tt * 2 + half:tt * 2 + half + 1]),
                              reads=[("ps", b)], writes=[("xnq", tt), ("ssy", tt)])
                    yield
                    sc.op("dve", lambda: DVE.tensor_tensor(out=vy[:, tt:tt + 1], in0=ssy[:, 2 * tt:2 * tt + 1], in1=ssy[:, 2 * tt + 1:2 * tt + 2], op=ALU.add),
                          reads=[("ssy", tt)], writes=[("vy", tt)])
                    sc.op("dve", lambda: DVE.tensor_scalar(out=vy[:, tt:tt + 1], in0=vy[:, tt:tt + 1], scalar1=1.0 / D, scalar2=EPS, op0=ALU.mult, op1=ALU.add),
                          reads=[("vy", tt)], writes=[("vy", tt)])
                    yield
                    sc.op("pool", lambda: POOL.tensor_tensor(out=ry[:, tt:tt + 1], in0=vy[:, tt:tt + 1], in1=mh[:, 0:1], op=ALU.pow),
                          reads=[("vy", tt), "mh"], writes=[("ry", tt)])
                    yield
                    z = tt
                    for half in range(2):
                        b = tt * 2 + half
                        hs = slice(half * 512, (half + 1) * 512)
                        sc.op("dve", lambda: DVE.scalar_tensor_tensor(out=ost[z][:, hs], in0=ps[:, b, :], scalar=ry[:, tt:tt + 1], in1=Gb[:, 1, hs],
                                                                      op0=ALU.mult, op1=ALU.mult),
                              reads=[("ps", b), ("ry", tt), ("Gb", 1)], writes=[("ost", z)])
                    yield
                    sc.op("pool", lambda: POOL.tensor_tensor(out=ost[z][:], in0=ost[z][:], in1=x1c[:, tt, :], op=ALU.add),
                          reads=[("ost", z), ("x1c", tt)], writes=[("ost", z)])
                    yield
                    r0 = (ck * 4 + tt) * 128
                    sc.dma("sp", lambda: SP.dma_start(out=out_d[r0:r0 + 128, :], in_=ost[z][:]), reads=[("ost", z)])

                gens_ = [epi(ck, tt) for tt in range(4)]
                if ck + 1 < 4:
                    gens_ += [pro(ck + 1, tt) for tt in range(4)]
                run_skewed(gens_)
            sc.barrier(final=True)


def build(stage_limit=99, dbg=False):
    nc0 = bass.Bass("TRN2", target_bir_lowering=False)
    s0 = Sched(nc0, None)
    program(nc0, s0, stage_limit, dbg)
    flags = s0.need
    nc = bass.Bass("TRN2", target_bir_lowering=False)
    s1 = Sched(nc, flags)
    program(nc, s1, stage_limit, dbg)
    return nc


def host_tables(inp):
    f = np.float32
    w_rg = inp["w_rg"][0]; conv_w = inp["conv_w"][0]; rpb = inp["rpb"][0]
    b_ada = inp["b_ada"][0]; g_norm = inp["g_norm"][0]; b_gate = inp["b_gate"][0]; b_rg = inp["b_rg"][0]; lam = inp["lam"][0]

    def fm(v):
        return np.ascontiguousarray(v.reshape(8, 128).T)

    t = {}
    t["badaFM"] = np.ascontiguousarray(b_ada.reshape(6, 8, 128).transpose(2, 0, 1)).astype(f)
    t["bc_rows"] = np.ascontiguousarray(np.stack([b_ada[2 * D:3 * D], b_ada[5 * D:6 * D], g_norm[1], g_norm[3]])).astype(f)
    vecs = [g_norm[0], g_norm[2], inp["conv_b"][0], b_gate[0:D], b_gate[D:2 * D], b_rg[0, 0], b_rg[0, 1], b_rg[1, 0], b_rg[1, 1], lam[0], lam[1]]
    t["fmvec"] = np.ascontiguousarray(np.stack([fm(v) for v in vecs], axis=1)).astype(f)
    wbd = np.zeros((128, 2, 2, 8, 128), f)
    for cc in range(8):
        for hl in range(2):
            wbd[hl * 64:(hl + 1) * 64, :, :, cc, hl * 64:(hl + 1) * 64] = w_rg[:, :, 2 * cc + hl].transpose(2, 0, 1, 3)
    t["wrgBD"] = np.ascontiguousarray(wbd.reshape(128, 32, 128))
    cvd = np.zeros((128, 4, 8, 128), f)
    pidx = np.arange(128)
    for tap in range(4):
        for cc in range(8):
            cvd[pidx, tap, cc, pidx] = conv_w[tap, cc * 128:(cc + 1) * 128]
    t["convD"] = np.ascontiguousarray(cvd.reshape(128, 32, 128))
    krel = np.arange(4)[:, None, None, None]; kcb = np.arange(32)[None, :, None, None]
    qrel = np.arange(12)[None, None, :, None]; qcb = np.arange(16)[None, None, None, :]
    dr = np.clip(krel - qrel + 4 + 7, 0, 14) + 0 * kcb + 0 * qcb
    dc = np.clip(kcb - qcb - 8 + 15, 0, 30) + 0 * krel + 0 * qrel
    rt = rpb[:, dr, dc]
    t["RT"] = np.ascontiguousarray(rt.reshape(16, 128, 192).transpose(1, 0, 2)).astype(f)
    ak = np.zeros((64, 4, 8, 4, 32), f)
    for j in range(4):
        for g in range(8):
            for qr in range(32):
                r0 = min(max(qr - 4, 0), 24)
                for kr_ in range(4):
                    kr = 4 * g + kr_
                    if not (r0 <= kr <= r0 + 7):
                        ak[qr, j, g, kr_, :] = NEG
            for qc_ in range(16):
                qc = 16 * j + qc_
                cs = min(max(qc - 8, 0), 48)
                for kc_ in range(32):
                    kc = 16 * j - 8 + kc_
                    if not ((0 <= kc < 64) and (cs <= kc < cs + 16)):
                        ak[32 + qc_, j, g, :, kc_] = NEG
    t["AK64"] = np.ascontiguousarray(ak.reshape(64, 32, 128))
    eq = np.zeros((64, 32, 64), f)
    for qr in range(32):
        eq[qr, qr, :] = 1.0
    for c_ in range(16):
        eq[32 + c_, :, c_::16] = 1.0
    t["EQ64"] = np.ascontiguousarray(eq.reshape(64, 2048))
    t["ident"] = np.eye(128, dtype=f)
    return t


def make_in_maps(inp):
    f = np.float32
    shared = host_tables(inp)
    for k in ("w_ada", "w_in", "w_lru_out", "w_na_out", "w_o", "w_mlp1", "w_mlp2"):
        shared[k] = np.ascontiguousarray(inp[k][0]).astype(f)
    maps = []
    cctx = inp["c_ctx"].reshape(8, 128).T
    for b in range(8):
        m = dict(shared)
        m["x"] = np.ascontiguousarray(inp["x"][b]).astype(f)
        m["ctx"] = np.ascontiguousarray(inp["ctx"][b]).astype(f)
        m["cc2"] = np.ascontiguousarray(np.stack([inp["c"][b].reshape(8, 128).T, cctx], axis=-1)).astype(f)
        maps.append(m)
    return maps


_NC = {}


def kernel(**inputs):
    inp = {k: np.asarray(v) for k, v in inputs.items()}
    if "nc" not in _NC:
        _NC["nc"] = build()
    maps = make_in_maps(inp)
    res = run_bass_kernel_spmd(_NC["nc"], maps, core_ids=list(range(8)))
    return np.stack([np.asarray(r["out"]) for r in res.results], axis=0).astype(np.float32)
```

```python
from contextlib import ExitStack
import numpy as np
import concourse.bass as bass
import concourse.mybir as mybir
from concourse.bass_utils import run_bass_kernel_spmd

F32 = mybir.dt.float32
BF16 = mybir.dt.bfloat16
AF = mybir.ActivationFunctionType
ALU = mybir.AluOpType

D = 1024
S = 2048
C = 256
LAT0 = 8
CTX0 = 2064
NTP = 2320
NL = 2304
EPS = 1e-6
NEG = -1e30
KD = 24
DBG = {"npairs": 8, "att": True, "nj": 4, "local": True, "norm": True, "kcopy": True, "vproj": True, "vtr": True, "vev": 3, "fastrec": False, "pro_skew": True, "epi_skew": True}
ENGS = ("pe", "act", "dve", "pool", "sp")


class Sched:
    def __init__(self, nc, flags=None):
        self.nc = nc
        self.emit = flags is not None
        self.flags = flags
        self.eng = {"pe": nc.tensor, "act": nc.scalar, "dve": nc.vector, "pool": nc.gpsimd, "sp": nc.sync}
        self.ids = {e: 0 for e in ENGS}
        self.sigcount = {e: 0 for e in ENGS}
        self.sigval = {e: {} for e in ENGS}
        self.need = {e: set() for e in ENGS}
        self.known = {e: {p: -1 for p in ENGS} for e in ENGS}
        self.kdma = {e: {} for e in ENGS}
        self.res = {}
        self.dma_n = 0
        self.sem = {}
        self.dsem = []
        self.n_wait = 0
        self.efree = {e: 0.0 for e in ENGS}
        self.rt_w = {}
        self.rt_r = {}
        self.dma_free = 0.0

    def est_start(self, eng, reads, writes):
        t = self.efree[eng]
        for r in reads:
            t = max(t, self.rt_w.get(r, 0.0) + 150.0)
        for w in writes:
            t = max(t, self.rt_w.get(w, 0.0) + 150.0, self.rt_r.get(w, 0.0) + 150.0)
        return t

    def account(self, eng, reads, writes, cost, is_dma=False):
        t0 = self.est_start(eng, reads, writes)
        if is_dma:
            self.efree[eng] = t0 + 100.0
            t0 = max(t0, self.dma_free)
            self.dma_free = t0 + cost
            t1 = t0 + cost + 2000.0
        else:
            t1 = t0 + cost
            self.efree[eng] = t1
        for r in reads:
            self.rt_r[r] = max(self.rt_r.get(r, 0.0), t1)
        for w in writes:
            self.rt_w[w] = t1
            self.rt_r[w] = 0.0

    def run_list(self, gens):
        pend = {}
        gens = list(gens)
        while gens:
            for g in list(gens):
                if pend.get(id(g)) is None:
                    try:
                        pend[id(g)] = next(g)
                    except StopIteration:
                        gens.remove(g)
                        pend.pop(id(g), None)
            best = None
            for g in gens:
                d = pend.get(id(g))
                if d is None:
                    continue
                t = self.est_start(d[0], d[2], d[3])
                if best is None or t < best[0]:
                    best = (t, g, d)
            if best is None:
                if gens:
                    continue_possible = False
                    for g in gens:
                        if pend.get(id(g)) is None:
                            continue_possible = True
                    if not continue_possible:
                        raise RuntimeError("list scheduler deadlock")
                    self._spin = getattr(self, "_spin", 0) + 1
                    if self._spin > 100000:
                        raise RuntimeError("list scheduler livelock")
                continue
            self._spin = 0
            _, g, d = best
            pend[id(g)] = None
            is_dma = len(d) > 5 and d[5] == "dma"
            self.account(d[0], d[2], d[3], d[4], is_dma)
            if is_dma:
                self.dma(d[0], d[1], reads=d[2], writes=d[3])
            else:
                self.op(d[0], d[1], reads=d[2], writes=d[3])

    def alloc_sems(self, stack):
        for e in ENGS:
            self.sem[e] = stack.enter_context(self.nc.semaphore("s_" + e))
        for i in range(KD):
            self.dsem.append(stack.enter_context(self.nc.semaphore("d_%d" % i)))

    def _st(self, r):
        st = self.res.get(r)
        if st is None:
            st = {"w": None, "rd": {}, "drd": []}
            self.res[r] = st
        return st

    def _wait(self, eng, dep):
        pe_, pid = dep
        if pe_ == "dma":
            slot = pid % KD
            if self.kdma[eng].get(slot, -1) >= pid:
                return
            self.kdma[eng][slot] = pid
            self.n_wait += 1
            if self.emit:
                self.eng[eng].wait_ge(self.dsem[slot], 16 * (pid // KD + 1))
        else:
            if (pe_ == eng and eng in ("pe", "sp")) or self.known[eng][pe_] >= pid:
                return
            self.known[eng][pe_] = pid
            self.n_wait += 1
            if self.emit:
                self.eng[eng].wait_ge(self.sem[pe_], self.sigval[pe_][pid])
            else:
                self.need[pe_].add(pid)

    def _deps(self, reads, writes, eng=None):
        deps = []
        for r in reads:
            st = self._st(r)
            if st["w"] is not None:
                deps.append(st["w"])
        for w in writes:
            st = self._st(w)
            if st["w"] is not None:
                deps.append(st["w"])
            deps.extend(d_ for d_ in st["rd"].items() if d_[0] != eng)
            deps.extend(("dma", n) for n in st["drd"])
        return deps

    def op(self, eng, fn, reads=(), writes=()):
        for dep in self._deps(reads, writes, eng):
            self._wait(eng, dep)
        my = self.ids[eng]
        self.ids[eng] += 1
        if self.emit:
            ins = fn()
            if my in self.flags[eng]:
                ins.then_inc(self.sem[eng], 1)
                self.sigcount[eng] += 1
                self.sigval[eng][my] = self.sigcount[eng]
        for r in reads:
            self._st(r)["rd"][eng] = my
        for w in writes:
            st = self._st(w)
            st["w"] = (eng, my)
            st["rd"] = {}
            st["drd"] = []

    def dma(self, q, fn, reads=(), writes=()):
        for dep in self._deps(reads, writes, q):
            self._wait(q, dep)
        n = self.dma_n
        self.dma_n += 1
        if n >= KD:
            self._wait(q, ("dma", n - KD))
        if self.emit:
            fn().then_inc(self.dsem[n % KD], 16)
        for r in reads:
            self._st(r)["drd"].append(n)
        for w in writes:
            st = self._st(w)
            st["w"] = ("dma", n)
            st["rd"] = {}
            st["drd"] = []

    def barrier(self, final=False):
        for e in ENGS:
            for p in ENGS:
                if p != e and self.ids[p] > 0:
                    self._wait(e, (p, self.ids[p] - 1))
            for n in range(max(0, self.dma_n - KD), self.dma_n):
                self._wait(e, ("dma", n))
        if not final:
            self.res = {}


def c_pe(n):
    return 64.0 + n / 2.0


def c_act(n):
    return 220.0 + n * 1.0


def c_dve(n, two=False):
    return 100.0 + n * (2.1 if two else 1.05)


def c_pool(n):
    return 300.0 + n * 1.9


def run_skewed(gens):
    it = iter(gens)
    active = []
    more = True
    while more or active:
        if more:
            try:
                active.append(next(it))
            except StopIteration:
                more = False
        for g in list(active):
            try:
                next(g)
            except StopIteration:
                active.remove(g)


class Arena:
    def __init__(self, big, nbytes):
        self.big = big
        self.free = [(0, nbytes)]

    def alloc(self, shape, dt_):
        esz = 4 if dt_ == F32 else 2
        n = 1
        for d_ in shape[1:]:
            n *= d_
        nb = (n * esz + 63) // 64 * 64
        for i, (o, sz) in enumerate(self.free):
            if sz >= nb:
                if sz == nb:
                    self.free.pop(i)
                else:
                    self.free[i] = (o + nb, sz - nb)
                ap = self.big[0:shape[0], o // 2:(o + n * esz) // 2]
                if dt_ == F32:
                    ap = ap.bitcast(F32)
                if len(shape) == 3:
                    ap = ap.rearrange("p (a b) -> p a b", a=shape[1], b=shape[2])
                elif len(shape) == 4:
                    ap = ap.rearrange("p (a b c) -> p a b c", a=shape[1], b=shape[2], c=shape[3])
                return ap, (o, nb)
        raise RuntimeError("arena out of SBUF: need %d, free=%s" % (nb, self.free))

    def release(self, h):
        self.free.append(h)
        self.free.sort()
        m = []
        for o, sz in self.free:
            if m and m[-1][0] + m[-1][1] == o:
                m[-1] = (m[-1][0], m[-1][1] + sz)
            else:
                m.append((o, sz))
        self.free = m


class Stage(ExitStack):
    def __init__(self, arena):
        super().__init__()
        self.arena = arena
        self.handles = []

    def __exit__(self, *a):
        for h in self.handles:
            self.arena.release(h)
        self.handles = []
        return super().__exit__(*a)

    def close(self):
        self.__exit__(None, None, None)


def program(nc, sc, stage_limit=99, dbg=False):
    PE, ACT, DVE, POOL, SP = nc.tensor, nc.scalar, nc.vector, nc.gpsimd, nc.sync

    def din(name, shape):
        return nc.dram_tensor(name, shape, F32, kind="ExternalInput").ap()

    x_d = din("x", [S, D]); ctx_d = din("ctx", [C, D]); cc2_d = din("cc2", [128, 8, 2])
    wada_d = din("w_ada", [D, 6 * D]); win_d = din("w_in", [D, 7 * D])
    wlo_d = din("w_lru_out", [D, D]); wno_d = din("w_na_out", [D, D]); wo_d = din("w_o", [D, D])
    w1_d = din("w_mlp1", [D, 4 * D]); w2_d = din("w_mlp2", [4 * D, D])
    bada_d = din("badaFM", [128, 6, 8]); bc_d = din("bc_rows", [4, D]); fmv_d = din("fmvec", [128, 11, 8])
    wrg_d = din("wrgBD", [128, 32, 128]); cvd_d = din("convD", [128, 32, 128])
    rt_d = din("RT", [128, 16, 192]); ak_d = din("AK64", [64, 32, 128]); eq_d = din("EQ64", [64, S])
    idn_d = din("ident", [128, 128])
    out_d = nc.dram_tensor("out", [S, D], F32, kind="ExternalOutput").ap()
    dbg_d = {}
    if dbg:
        for nm, shp, dt_ in (("d_uT", [128, 8, NTP], BF16), ("d_hg", [128, 8, S], BF16), ("d_oT", [128, 8, S], BF16),
                             ("d_ym", [128, 8, S], BF16)):
            dbg_d[nm] = nc.dram_tensor(nm, shp, dt_, kind="ExternalOutput").ap()

    def wview(w, c0, n):
        return w[:, c0:c0 + n].rearrange("(kc p) n -> p kc n", p=128)

    ARENA_BYTES = 206 * 1024
    with ExitStack() as top0:
        sc.alloc_sems(top0)
        big = top0.enter_context(nc.sbuf_tensor("sb_big", [128, ARENA_BYTES // 2], BF16))
        arena = Arena(big, ARENA_BYTES)
        top = top0.enter_context(Stage(arena))

        def sb(st, name, shape, dt_):
            ap, h = arena.alloc(shape, dt_)
            st.handles.append(h)
            return ap

        ident = sb(top, "ident", [128, 128], BF16)
        ones = sb(top, "ones", [128, 128], BF16)
        fmv = sb(top, "fmv", [128, 11, 8], F32)
        hbv = sb(top, "hbv", [128, 4, 8], F32)
        hcv = sb(top, "hcv", [128, 2, 8], F32)
        dm = sb(top, "dm", [128, 6, 8], F32)
        Gb = sb(top, "Gb", [128, 2, D], F32)
        mh = sb(top, "mh", [128, 32], F32)
        mid = Stage(arena)
        uT = sb(mid, "uT", [128, 8, NTP], BF16)
        scb = sb(mid, "scb", [128, 8, 2], BF16); rep = sb(mid, "rep", [128, 8, 128], BF16)
        bada = sb(mid, "bada", [128, 6, 8], F32); mraw = sb(mid, "mraw", [128, 4, 8, 2], F32)
        fmidx = {0: 0, 1: 1, 3: 2, 4: 3}

        def ada_group(g, ps, w, wn, bcr, bcn, r_b, r_g):
            if g in fmidx:
                gi = fmidx[g]
                for oc in range(8):
                    for kc in range(8):
                        sc.op("pe", lambda: PE.matmul(ps[:, 0, (gi * 8 + oc) * 2:(gi * 8 + oc) * 2 + 2],
                                                      lhsT=w[:, kc, oc * 128:(oc + 1) * 128], rhs=scb[:, kc, :],
                                                      start=(kc == 0), stop=(kc == 7)),
                              reads=[wn, "scb"], writes=[("ps", 0)])
                pv = ps[:, 0, 0:64].rearrange("p (g o t) -> p g o t", g=4, o=8, t=2)
                for col in range(2):
                    sc.op("dve", lambda: DVE.tensor_tensor(out=mraw[:, gi, :, col], in0=pv[:, gi, :, col], in1=bada[:, g, :],
                                                           op=ALU.add), reads=[("ps", 0), "bada"], writes=["mraw"])
            else:
                bi = 0 if g == 2 else 1
                for half in range(2):
                    for kc in range(8):
                        sc.op("pe", lambda: PE.matmul(ps[:, 1 + half, :], lhsT=rep[:, kc, :], rhs=w[:, kc, half * 512:(half + 1) * 512],
                                                      start=(kc == 0), stop=(kc == 7)),
                              reads=[wn, "rep"], writes=[("ps", 1 + half)])
                    hs = slice(half * 512, (half + 1) * 512)
                    sc.op("dve", lambda: DVE.tensor_tensor(out=Gb[:, bi, hs], in0=ps[:, 1 + half, :], in1=bcr[:, r_b, hs],
                                                           op=ALU.add), reads=[("ps", 1 + half), bcn], writes=[("Gb", bi)])
                    sc.op("dve", lambda: DVE.tensor_tensor(out=Gb[:, bi, hs], in0=Gb[:, bi, hs], in1=bcr[:, r_g, hs],
                                                           op=ALU.mult), reads=[bcn], writes=[("Gb", bi)])

        with Stage(arena) as st:
            ps = st.enter_context(nc.psum_tensor("ps0", [128, 8, 512], F32))
            c2 = sb(st, "c2", [128, 8, 2], F32); sl = sb(st, "sl", [128, 8, 2], F32)
            wa = [sb(st, "wa%d" % i, [128, 8, D], BF16) for i in range(2)]
            xall = sb(st, "xall", [128, 6, D], F32)
            junk = sb(st, "junk", [128, D], BF16)
            ss = sb(st, "ss", [128, 18], F32); vv = sb(st, "vv", [128, 18], F32); rstd = sb(st, "rstd", [128, 18], F32)
            xnb = [sb(st, "xnb%d" % i, [128, D], BF16) for i in range(4)]
            lamt = sb(st, "lamt", [128, 2, 8], F32)

            sc.dma("sp", lambda: SP.dma_start(out=c2[:], in_=cc2_d), writes=["c2"])
            sc.dma("sp", lambda: SP.dma_start(out=fmv[:], in_=fmv_d), writes=["fmv"])
            sc.dma("sp", lambda: SP.dma_start(out=bada[:], in_=bada_d), writes=["bada"])
            sc.dma("pool", lambda: POOL.dma_start(out=ident[:], in_=idn_d), writes=["ident"])
            order = [0, 1, 2, 3, 4, 5]
            sc.dma("pool", lambda: POOL.dma_start(out=wa[0][:], in_=wview(wada_d, 0, D)), writes=["wa0"])
            sc.dma("pool", lambda: POOL.dma_start(out=wa[1][:], in_=wview(wada_d, D, D)), writes=["wa1"])

            sc.op("dve", lambda: DVE.memset(ones[:], 1.0), writes=["ones"])
            sc.op("dve", lambda: DVE.memset(mh[:], -0.5), writes=["mh"])
            sc.op("dve", lambda: DVE.memset(uT[:, :, 0:LAT0], 0.0), writes=["uTpad"])
            sc.op("dve", lambda: DVE.memset(uT[:, :, LAT0 + S:CTX0], 0.0), writes=["uTpad"])
            sc.op("dve", lambda: DVE.tensor_scalar(out=hbv[:], in0=fmv[:, 5:9, :], scalar1=0.5, scalar2=None, op0=ALU.mult),
                  reads=["fmv"], writes=["hbv"])
            sc.op("act", lambda: ACT.activation(out=lamt[:], in_=fmv[:, 9:11, :], func=AF.Exp, scale=-1.0), reads=["fmv"], writes=["lamt"])
            sc.op("act", lambda: ACT.activation(out=lamt[:], in_=lamt[:], func=AF.Ln, bias=1.0, scale=1.0), reads=["lamt"], writes=["lamt"])
            sc.op("dve", lambda: DVE.tensor_scalar(out=hcv[:], in0=lamt[:], scalar1=-4.0, scalar2=None, op0=ALU.mult),
                  reads=["lamt"], writes=["hcv"])
            sc.op("act", lambda: ACT.activation(out=sl[:], in_=c2[:], func=AF.Silu), reads=["c2"], writes=["sl"])
            sc.op("dve", lambda: DVE.tensor_copy(out=scb[:], in_=sl[:]), reads=["sl"], writes=["scb"])
            for kc in range(8):
                sc.op("dve", lambda kc=kc: DVE.tensor_scalar(out=rep[:, kc, :], in0=ones[:], scalar1=sl[:, kc, 0:1], scalar2=None,
                                                           op0=ALU.mult), reads=["sl", "ones"], writes=["rep"])

            ada_group(0, ps, wa[0], "wa0", None, None, 0, 0)
            ada_group(1, ps, wa[1], "wa1", None, None, 0, 0)
            sc.op("dve", lambda: DVE.scalar_tensor_tensor(out=dm[:, 0, :], in0=mraw[:, 1, :, 0], scalar=1.0, in1=fmv[:, 0, :],
                                                          op0=ALU.add, op1=ALU.mult), reads=["mraw", "fmv"], writes=["dm"])
            sc.op("dve", lambda: DVE.tensor_copy(out=dm[:, 1, :], in_=mraw[:, 0, :, 0]), reads=["mraw"], writes=["dm"])
            sc.op("dve", lambda: DVE.scalar_tensor_tensor(out=dm[:, 2, :], in0=mraw[:, 1, :, 1], scalar=1.0, in1=fmv[:, 0, :],
                                                          op0=ALU.add, op1=ALU.mult), reads=["mraw", "fmv"], writes=["dm"])
            sc.op("dve", lambda: DVE.tensor_copy(out=dm[:, 3, :], in_=mraw[:, 0, :, 1]), reads=["mraw"], writes=["dm"])

            def xload(tt):
                src = x_d[tt * 128:(tt + 1) * 128, :] if tt < 16 else ctx_d[(tt - 16) * 128:(tt - 15) * 128, :]
                sc.dma("sp", lambda: SP.dma_start(out=xall[:, tt % 6, :], in_=src), writes=[("xall", tt % 6)])

            for tt in range(5):
                xload(tt)
            groups = [(0, 4, LAT0), (4, 4, LAT0 + 512), (8, 4, LAT0 + 1024), (12, 4, LAT0 + 1536), (16, 2, CTX0)]
            tpb = [ps[:, 4 + h, :].bitcast(BF16).rearrange("p (k t) -> p k t", k=2) for h in range(4)]
            for gi_, (t0, nt, col0) in enumerate(groups):
                for j in range(nt):
                    tt = t0 + j
                    if tt + 5 < 18:
                        xload(tt + 5)
                    xs = tt % 6
                    slot = tt % 4
                    sc.op("act", lambda: ACT.activation(out=junk[:], in_=xall[:, xs, :], func=AF.Square, accum_out=ss[:, tt:tt + 1]),
                          reads=[("xall", xs)], writes=["junk", ("ss", tt)])
                    sc.op("dve", lambda: DVE.tensor_scalar(out=vv[:, tt:tt + 1], in0=ss[:, tt:tt + 1], scalar1=1.0 / D, scalar2=EPS, op0=ALU.mult, op1=ALU.add),
                          reads=[("ss", tt)], writes=[("vv", tt)])
                    sc.op("pool", lambda: POOL.tensor_tensor(out=rstd[:, tt:tt + 1], in0=vv[:, tt:tt + 1], in1=mh[:, 0:1], op=ALU.pow),
                          reads=[("vv", tt), "mh"], writes=[("rstd", tt)])
                    sc.op("act", lambda: ACT.activation(out=xnb[slot][:], in_=xall[:, xs, :], func=AF.Identity, scale=rstd[:, tt:tt + 1]),
                          reads=[("xall", xs), ("rstd", tt)], writes=[("xnb", slot)])
                    for kc in range(8):
                        sc.op("pe", lambda: PE.transpose(out=tpb[kc // 2][:, kc % 2, j * 128:(j + 1) * 128],
                                                         in_=xnb[slot][:, kc * 128:(kc + 1) * 128], identity=ident[:]),
                              reads=[("xnb", slot), "ident"], writes=[("ps", 4 + kc // 2)])
                si = 0 if t0 < 16 else 2
                for kc in range(8):
                    n = nt * 128
                    sc.op("dve", lambda: DVE.tensor_scalar(
                        out=uT[:, kc, col0:col0 + n], in0=tpb[kc // 2][:, kc % 2, 0:n], scalar1=dm[:, si, kc:kc + 1],
                        scalar2=dm[:, si + 1, kc:kc + 1], op0=ALU.mult, op1=ALU.add),
                        reads=[("ps", 4 + kc // 2), "dm"], writes=[("uT", kc)])
            if dbg:
                sc.dma("sp", lambda: SP.dma_start(out=dbg_d["d_uT"], in_=uT[:]), reads=[("uT", k) for k in range(8)] + ["uTpad"])
            sc.barrier()
        if stage_limit <= 0:
            sc.barrier(final=True)
            mid.close()
            return

        hgT = sb(mid, "hgT", [128, 8, S], BF16)
        pieces = [(512 * p, 512, LAT0 + 512 * p, 0, S) for p in range(4)] + [(S, C, CTX0, S, S + C)]
        with Stage(arena) as st:
            ps = st.enter_context(nc.psum_tensor("ps1", [128, 8, 512], F32))
            wxs = [sb(st, "wxs%d" % i, [128, 8, 128], BF16) for i in range(2)]; wgs = [sb(st, "wgs%d" % i, [128, 8, 128], BF16) for i in range(2)]
            wrgs = [sb(st, "wrgs%d" % i, [128, 4, 128], BF16) for i in range(2)]; cvds = [sb(st, "cvds%d" % i, [128, 4, 128], BF16) for i in range(2)]
            xrb = [sb(st, "xrb%d" % i, [128, NL], BF16) for i in range(2)]
            xcb = [sb(st, "xcb%d" % i, [128, NL], BF16) for i in range(2)]
            trb = [[sb(st, "tr%d_%d" % (i, d), [128, NL], F32) for d in range(2)] for i in range(2)]
            tib = [sb(st, "ti%d" % i, [128, NL], F32) for i in range(2)]
            avb = [sb(st, "av%d" % i, [128, NL], F32) for i in range(2)]
            a2b = [sb(st, "a2%d" % i, [128, NL], F32) for i in range(2)]
            wrg4 = wrg_d.rearrange("p (a c) m -> p a c m", c=8)
            cvd4 = cvd_d.rearrange("p (a c) m -> p a c m", c=8)
            bk = [0, 0]

            def nbz(z):
                b = 4 * z + bk[z] % 4
                bk[z] += 1
                return b

            def front(cc):
                z = cc % 2
                yield ("pool", lambda: POOL.dma_start(out=wxs[z][:], in_=wview(win_d, cc * 128, 128)), [], [("wxs", z)], 1500.0, "dma")
                yield ("pool", lambda: POOL.dma_start(out=cvds[z][:], in_=cvd4[:, :, cc, :]), [], [("cvds", z)], 500.0, "dma")
                yield ("pool", lambda: POOL.dma_start(out=wrgs[z][:], in_=wrg4[:, :, cc, :]), [], [("wrgs", z)], 500.0, "dma")
                yield ("pool", lambda: POOL.dma_start(out=wgs[z][:], in_=wview(win_d, D + cc * 128, 128)), [], [("wgs", z)], 1500.0, "dma")
                for (l0, n, u0, lo, hi) in pieces:
                    b = nbz(z)
                    for kc in range(8):
                        yield ("pe", lambda: PE.matmul(ps[:, b, 0:n], lhsT=wxs[z][:, kc, :], rhs=uT[:, kc, u0:u0 + n], start=(kc == 0), stop=(kc == 7)),
                               [("wxs", z), ("uT", kc)], [("ps", b)], c_pe(n))
                    yield ("dve", lambda: DVE.tensor_copy(out=xrb[z][:, l0:l0 + n], in_=ps[:, b, 0:n]), [("ps", b)], [("xrb", z, l0)], c_dve(n))
                for (l0, n, u0, lo, hi) in pieces:
                    b = nbz(z)
                    for ti_, tap in enumerate([2, 0, 1, 3]):
                        o = tap - 2
                        ta = max(l0, lo - o); tb = min(l0 + n, hi - o)
                        yield ("pe", lambda: PE.matmul(ps[:, b, ta - l0:tb - l0], lhsT=cvds[z][:, tap, :], rhs=xrb[z][:, ta + o:tb + o],
                                                       start=(ti_ == 0), stop=(ti_ == 3)),
                               [("cvds", z)] + [("xrb", z, q_[0]) for q_ in pieces], [("ps", b)], c_pe(n))
                    yield ("dve", lambda: DVE.tensor_scalar(out=xcb[z][:, l0:l0 + n], in0=ps[:, b, 0:n], scalar1=fmv[:, 2, cc:cc + 1], scalar2=None,
                                                            op0=ALU.add), [("ps", b), "fmv"], [("xcb", z, l0)], c_dve(n))

            def back(cc):
                z = cc % 2
                ti, av, a2 = tib[z], avb[z], a2b[z]
                ge = xrb[z]
                xcb_all = [("xcb", z, q_[0]) for q_ in pieces]
                xrb_all = [("xrb", z, q_[0]) for q_ in pieces]
                for d in range(2):
                    tr = trb[z][d]
                    for (l0, n, u0, lo, hi) in pieces:
                        for g_ in range(2):
                            b = nbz(z)
                            yield ("pe", lambda: PE.matmul(ps[:, b, 0:n], lhsT=wrgs[z][:, d * 2 + g_, :], rhs=xcb[z][:, l0:l0 + n], start=True, stop=True),
                                   [("wrgs", z), ("xcb", z, l0)], [("ps", b)], c_pe(n))
                            dst = tr if g_ == 0 else ti
                            yield ("act", lambda: ACT.activation(out=dst[:, l0:l0 + n], in_=ps[:, b, 0:n], func=AF.Tanh,
                                                                 bias=hbv[:, d * 2 + g_, cc:cc + 1], scale=0.5),
                                   [("ps", b), "hbv"], [("tr", z, d), ("hs", z, d)] if g_ == 0 else [("ti", z)] + [("bb", z, q_) for q_ in range(3)], c_act(n))
                    rngs = [(S, NL), (0, 1024), (1024, S)] if d == 0 else [(S, NL), (1024, S), (0, 1024)]
                    for ri, (r0, r1) in enumerate(rngs):
                        yield ("act", lambda: ACT.activation(out=av[:, r0:r1], in_=tr[:, r0:r1], func=AF.Exp, bias=hcv[:, d, cc:cc + 1],
                                                             scale=hcv[:, d, cc:cc + 1]),
                               [("tr", z, d), "hcv"], [("av", z, ri)], c_act(r1 - r0) + (1300.0 if ri == 0 else 0.0))
                    for ri, (r0, r1) in enumerate(rngs):
                        yield ("pool", lambda: POOL.tensor_tensor(out=a2[:, r0:r1], in0=av[:, r0:r1], in1=av[:, r0:r1], op=ALU.mult),
                               [("av", z, ri)], [("a2", z, ri)], c_pool(r1 - r0))
                    for ri, (r0, r1) in enumerate(rngs):
                        yield ("act", lambda: ACT.activation(out=a2[:, r0:r1], in_=a2[:, r0:r1], func=AF.Sqrt, bias=1.0, scale=-1.0),
                               [("a2", z, ri)], [("a2", z, ri)], c_act(r1 - r0) + (1300.0 if ri == 0 else 0.0))
                    for ri, (r0, r1) in enumerate(rngs):
                        yield ("pool", lambda: POOL.tensor_tensor(out=a2[:, r0:r1], in0=a2[:, r0:r1], in1=xcb[z][:, r0:r1], op=ALU.mult),
                               [("a2", z, ri)] + xcb_all, [("a2", z, ri)], c_pool(r1 - r0))
                    for ri, (r0, r1) in enumerate(rngs):
                        yield ("dve", lambda: DVE.scalar_tensor_tensor(out=ti[:, r0:r1], in0=ti[:, r0:r1], scalar=1.0, in1=a2[:, r0:r1],
                                                                       op0=ALU.add, op1=ALU.mult),
                               [("ti", z), ("a2", z, ri)], [("bb", z, ri)], c_dve(r1 - r0, True))
                        rd = [("av", z, ri), ("bb", z, ri), ("hs", z, d)]
                        wr_ = [("hs", z, d)]
                        if ri == 0:
                            init = 0.0
                        elif d == 0:
                            init = tr[:, rngs[ri - 1][1] - 1:rngs[ri - 1][1]]
                        else:
                            init = tr[:, rngs[ri - 1][0]:rngs[ri - 1][0] + 1]
                        if d == 0:
                            yield ("dve", lambda: DVE.tensor_tensor_scan(out=tr[:, r0:r1], data0=av[:, r0:r1], data1=ti[:, r0:r1], initial=init,
                                                                         op0=ALU.mult, op1=ALU.add), rd, wr_, c_dve(r1 - r0, True))
                        else:
                            yield ("dve", lambda: DVE.tensor_tensor_scan(out=tr[:, r0:r1][:, ::-1], data0=av[:, r0:r1][:, ::-1],
                                                                         data1=ti[:, r0:r1][:, ::-1], initial=init,
                                                                         op0=ALU.mult, op1=ALU.add), rd, wr_, c_dve(r1 - r0, True))
                for p in range(4):
                    b = nbz(z)
                    for kc in range(8):
                        yield ("pe", lambda: PE.matmul(ps[:, b, :], lhsT=wgs[z][:, kc, :], rhs=uT[:, kc, LAT0 + 512 * p:LAT0 + 512 * (p + 1)],
                                                       start=(kc == 0), stop=(kc == 7)), [("wgs", z), ("uT", kc)], [("ps", b)], c_pe(512))
                    yield ("act", lambda: ACT.activation(out=ge[:, 512 * p:512 * (p + 1)], in_=ps[:, b, :], func=AF.Gelu_apprx_tanh),
                           [("ps", b)], [("xrb", z, 512 * p)], c_act(512) + (1300.0 if p == 0 else 0.0))
                yield ("dve", lambda: DVE.tensor_tensor(out=trb[z][1][:, 0:S], in0=trb[z][0][:, 0:S], in1=trb[z][1][:, 0:S], op=ALU.add),
                       [("hs", z, 0), ("hs", z, 1)], [("hs", z, 1)], c_dve(S, True))
                yield ("dve", lambda: DVE.scalar_tensor_tensor(out=hgT[:, cc, :], in0=trb[z][1][:, 0:S], scalar=0.5, in1=ge[:, 0:S], op0=ALU.mult,
                                                               op1=ALU.mult), [("hs", z, 1)] + xrb_all, [("hgT", cc)], c_dve(S, True))

            def stream(z):
                for cc in range(z, 8, 2):
                    yield from front(cc)
                    yield from back(cc)

            sc.run_list([stream(0), stream(1)])
            if dbg:
                sc.dma("sp", lambda: SP.dma_start(out=dbg_d["d_hg"], in_=hgT[:]), reads=[("hgT", k) for k in range(8)])
            sc.barrier()
        if stage_limit <= 1:
            sc.barrier(final=True)
            mid.close()
            return

        oT = sb(mid, "oT", [128, 8, S], BF16)
        with Stage(arena) as st:
            ps = st.enter_context(nc.psum_tensor("ps2", [128, 8, 512], F32))
            RT = sb(st, "RT", [128, 16, 192], F32)
            wq = [sb(st, "wq%d" % i, [128, 8, 128], BF16) for i in range(2)]
            wk = [sb(st, "wk%d" % i, [128, 8, 128], BF16) for i in range(2)]
            wv = [sb(st, "wv%d" % i, [128, 8, 128], BF16) for i in range(2)]
            qTA = sb(st, "qTA", [128, S], BF16); qTB = sb(st, "qTB", [128, S], BF16)
            kT = [sb(st, "kT0", [128, NTP], BF16)] * 2
            kTbA = sb(st, "kTbA", [128, 32, 128], BF16); kTbB = sb(st, "kTbB", [128, 32, 128], BF16)
            vT = sb(st, "vT", [128, 34 * 128], BF16)
            VV = sb(st, "VV", [128, 34, 192], BF16)
            pt = [sb(st, "pt%d" % i, [128, 512], BF16) for i in range(7)]
            sbf = [sb(st, "sbf%d" % i, [128, 192], F32) for i in range(6)]
            rec = [sb(st, "rec%d" % i, [128, 512], F32) for i in range(4)]
            sring = [0]
            sc.dma("sp", lambda: SP.dma_start(out=RT[:], in_=rt_d), writes=["RT"])
            sc.dma("pool", lambda: POOL.dma_start(out=qTA[64:128, :], in_=eq_d), writes=["qTAc"])
            sc.dma("pool", lambda: POOL.dma_start(out=qTB[0:64, :], in_=eq_d), writes=["qTBc"])
            sc.dma("pool", lambda: POOL.dma_start(out=kTbA[64:128, :, :], in_=ak_d), writes=["kTbAc"])
            sc.dma("pool", lambda: POOL.dma_start(out=kTbB[0:64, :, :], in_=ak_d), writes=["kTbBc"])
            for i in range(1):
                sc.op("dve", lambda i=i: DVE.memset(kT[i][:, 0:LAT0], 0.0), writes=[("kT", i)])
                sc.op("dve", lambda i=i: DVE.memset(kT[i][:, LAT0 + S:CTX0], 0.0), writes=[("kT", i)])
                sc.op("dve", lambda i=i: DVE.memset(VV[:, :, 64:128], 1.0), writes=[("VV", 0)])

            def load_w(i):
                s_ = i % 2
                sc.dma("pool", lambda: POOL.dma_start(out=wq[s_][:], in_=wview(win_d, 2 * D + i * 128, 128)), writes=[("wq", s_)])
                sc.dma("pool", lambda: POOL.dma_start(out=wk[s_][:], in_=wview(win_d, 3 * D + i * 128, 128)), writes=[("wk", s_)])
                sc.dma("pool", lambda: POOL.dma_start(out=wv[s_][:], in_=wview(win_d, 4 * D + i * 128, 128)), writes=[("wv", s_)])

            load_w(0)
            mmb = [0]

            def nmm():
                b = mmb[0]
                mmb[0] = (b + 1) % 4
                return b

            lcb = [0]
            r3 = [0]
            for i in range(DBG["npairs"]):
                s_ = i % 2
                if i + 1 < 8:
                    load_w(i + 1)
                for p in range(4):
                    b = nmm()
                    for kc in range(8):
                        sc.op("pe", lambda b=b, kc=kc, p=p: PE.matmul(ps[:, b, :], lhsT=wq[s_][:, kc, :], rhs=uT[:, kc, LAT0 + 512 * p:LAT0 + 512 * (p + 1)],
                                                                     start=(kc == 0), stop=(kc == 7)), reads=[("wq", s_), ("uT", kc)], writes=[("ps", b)])
                    sc.op("act", lambda b=b, p=p: ACT.activation(out=qTA[0:64, 512 * p:512 * (p + 1)], in_=ps[0:64, b, :], func=AF.Identity, scale=0.125),
                          reads=[("ps", b)], writes=[("qT", 0)])
                    sc.op("act", lambda b=b, p=p: ACT.activation(out=qTB[64:128, 512 * p:512 * (p + 1)], in_=ps[64:128, b, :], func=AF.Identity, scale=0.125),
                          reads=[("ps", b)], writes=[("qT", 1)])
                for (l0, n, u0, lo, hi) in pieces:
                    b = nmm()
                    for kc in range(8):
                        sc.op("pe", lambda b=b, kc=kc, n=n, u0=u0: PE.matmul(ps[:, b, 0:n], lhsT=wk[s_][:, kc, :], rhs=uT[:, kc, u0:u0 + n],
                                                                            start=(kc == 0), stop=(kc == 7)), reads=[("wk", s_), ("uT", kc)], writes=[("ps", b)])
                    sc.op("dve", lambda b=b, n=n, u0=u0: DVE.tensor_copy(out=kT[s_][:, u0:u0 + n], in_=ps[:, b, 0:n]),
                          reads=[("ps", b)], writes=[("kT", 0)])
                for j in range(4 if DBG["kcopy"] else 0):
                    base = LAT0 + 16 * j - 8
                    for (kt_, hp_, nm_) in ((kTbA, slice(0, 64), 0), (kTbB, slice(64, 128), 1)):
                        sc.op("pool", lambda: POOL.tensor_copy(
                            out=kt_[hp_, j * 8:(j + 1) * 8, :].rearrange("p g (r c) -> p g r c", r=4, c=32),
                            in_=kT[s_][hp_, base:base + 2048].rearrange("p (g r c) -> p g r c", g=8, r=4, c=64)[:, :, :, 0:32]),
                            reads=[("kT", 0)], writes=[("kTb", nm_)])
                for b4 in range(9 if DBG["vproj"] else 0):
                    b = nmm()
                    blks = list(range(b4 * 4, min(b4 * 4 + 4, 34)))
                    for bi_, blk in enumerate(blks):
                        for kc in range(8):
                            if blk < 32:
                                j, g = blk // 8, blk % 8
                                base = LAT0 + 4 * g * 64 + 16 * j - 8
                                mv = uT[:, kc, base:base + 256].rearrange("p (r c) -> p r c", c=64)[:, :, 0:32]
                            else:
                                t = blk - 32
                                mv = uT[:, kc, CTX0 + 128 * t:CTX0 + 128 * (t + 1)]
                            sc.op("pe", lambda: PE.matmul(ps[:, b, bi_ * 128:(bi_ + 1) * 128], lhsT=wv[s_][:, kc, :], rhs=mv,
                                                          start=(kc == 0), stop=(kc == 7)),
                                  reads=[("wv", s_), ("uT", kc)], writes=[("ps", b)])
                    nb_ = len(blks)
                    sc.op("act", lambda: ACT.activation(out=vT[:, blks[0] * 128:(blks[0] + nb_) * 128], in_=ps[:, b, 0:nb_ * 128], func=AF.Identity),
                          reads=[("ps", b)], writes=[("vT", b4)])
                for b8 in range(5 if DBG["vtr"] else 0):
                    b = 4 + b8 % 2
                    blks = list(range(b8 * 8, min(b8 * 8 + 8, 34)))
                    tpv = ps[:, b, :].bitcast(BF16).rearrange("p (k d) -> p k d", d=128)
                    for bi_, blk in enumerate(blks):
                        sc.op("pe", lambda: PE.transpose(out=tpv[:, bi_, :], in_=vT[:, blk * 128:(blk + 1) * 128], identity=ident[:]),
                              reads=[("vT", blk // 4), "ident"], writes=[("ps", b)])
                    nb_ = len(blks)
                    sc.op("dve", lambda: DVE.tensor_copy(
                        out=VV[:, blks[0]:blks[0] + nb_, :].rearrange("p k (a c) -> p k a c", c=64)[:, :, 0::2, :],
                        in_=tpv[:, 0:nb_, :].rearrange("p k (a c) -> p k a c", c=64)), reads=[("ps", b)], writes=[("VV", 0)])
                def unit(j, hh, kind, idx):
                    hp = slice(0, 64) if hh == 0 else slice(64, 128)
                    Vx = VV[:, :, 0:128] if hh == 0 else VV[:, :, 64:192]
                    h = 2 * i + hh
                    ab = 4 + (j % 2) * 2 + hh
                    qx = qTA if hh == 0 else qTB
                    q3 = qx[hp, :].rearrange("p (r c) -> p r c", c=64)
                    q3f = qx[:, :].rearrange("p (r c) -> p r c", c=64)
                    kx = kTbA if hh == 0 else kTbB
                    b = sring[0] % 4
                    sring[0] += 1
                    if kind == "c":
                        nq, c0, vblk = 512, 0, 32 + idx
                        sc.op("pe", lambda: PE.matmul(ps[:, b, :], lhsT=kT[s_][hp, CTX0 + 128 * idx:CTX0 + 128 * (idx + 1)],
                                                      rhs=q3[:, :, 16 * j:16 * j + 16], start=True, stop=True),
                              reads=[("kT", 0), ("qT", hh)], writes=[("ps", b)])
                    else:
                        g = idx
                        lo_ = max(0, 4 - 4 * g); hi_ = min(12, 36 - 4 * g)
                        nq = (hi_ - lo_) * 16
                        qr0 = 4 * g - 4 + lo_
                        c0, vblk = qr0 * 16, j * 8 + g
                        sc.op("pe", lambda: PE.matmul(ps[:, b, 0:nq], lhsT=kx[:, j * 8 + g, :],
                                                      rhs=q3f[:, qr0:qr0 + (hi_ - lo_), 16 * j:16 * j + 16], start=True, stop=True),
                              reads=[("kTb", hh), ("qT", hh), "qTAc", "qTBc", "kTbAc", "kTbBc"], writes=[("ps", b)])
                    yield
                    yield
                    r_ = r3[0] % 7
                    rs_ = r3[0] % 6
                    r3[0] += 1
                    if kind == "c":
                        sc.op("act", lambda: ACT.activation(out=pt[r_][:], in_=ps[:, b, :], func=AF.Exp),
                              reads=[("ps", b)], writes=[("pt", r_)])
                    else:
                        sc.op("dve", lambda: DVE.tensor_tensor(out=sbf[rs_][:, 0:nq], in0=ps[:, b, 0:nq], in1=RT[:, h, lo_ * 16:hi_ * 16], op=ALU.add),
                              reads=[("ps", b), "RT"], writes=[("sbf", rs_)])
                        yield
                        sc.op("act", lambda: ACT.activation(out=pt[r_][:, 0:nq], in_=sbf[rs_][:, 0:nq], func=AF.Exp),
                              reads=[("sbf", rs_)], writes=[("pt", r_)])
                    yield
                    yield
                    yield
                    yield
                    first = (kind == "c" and idx == 0)
                    last = (kind == "l" and idx == 7)
                    sc.op("pe", lambda: PE.matmul(ps[:, ab, c0:c0 + nq], lhsT=Vx[:, vblk, :], rhs=pt[r_][:, 0:nq], start=first, stop=last),
                          reads=[("VV", 0), ("pt", r_)], writes=[("ps", ab)])
                    if last:
                        yield
                        op_ = slice(0, 64) if hh == 0 else slice(64, 128)
                        dp_ = slice(64, 128) if hh == 0 else slice(0, 64)
                        rc = rec[(j % 2) * 2 + hh]
                        if DBG["fastrec"]:
                            sc.op("dve", lambda: DVE.reciprocal_approx_fast(out=rc[op_, :], in_=ps[dp_, ab, :]),
                                  reads=[("ps", ab)], writes=[("rec", (j % 2) * 2 + hh)])
                        else:
                            sc.op("dve", lambda: DVE.reciprocal(out=rc[op_, :], in_=ps[dp_, ab, :]),
                                  reads=[("ps", ab)], writes=[("rec", (j % 2) * 2 + hh)])
                        sc.op("dve", lambda: DVE.tensor_tensor(
                            out=oT[op_, i, :].rearrange("p (r c) -> p r c", c=64)[:, :, 16 * j:16 * j + 16],
                            in0=ps[op_, ab, :].rearrange("p (r c) -> p r c", c=16), in1=rc[op_, :].rearrange("p (r c) -> p r c", c=16),
                            op=ALU.mult), reads=[("ps", ab), ("rec", (j % 2) * 2 + hh)], writes=[("oT", i)])

                units = []
                for j in range(DBG["nj"] if DBG["att"] else 0):
                    for kind, idx in [("c", 0), ("c", 1)] + [("l", g) for g in range(8)]:
                        units += [unit(j, 0, kind, idx), unit(j, 1, kind, idx)]
                run_skewed(units)
            if dbg:
                sc.dma("sp", lambda: SP.dma_start(out=dbg_d["d_oT"], in_=oT[:]), reads=[("oT", k) for k in range(8)])
            sc.barrier()
        if stage_limit <= 2:
            sc.barrier(final=True)
            mid.close()
            return

        ymT = sb(top, "ymT", [128, 8, S], BF16)
        with Stage(arena) as st:
            ps = st.enter_context(nc.psum_tensor("ps3", [128, 8, 512], F32))
            wsl = [[sb(st, "wsl%d_%d" % (i, k), [128, 8, 128], BF16) for k in range(4)] for i in range(2)]
            gA = [sb(st, "gA%d" % i, [128, 512], F32) for i in range(2)]
            gB = [sb(st, "gB%d" % i, [128, 512], F32) for i in range(2)]
            t1 = [sb(st, "t1%d" % i, [128, 512], F32) for i in range(2)]
            t2 = [sb(st, "t2%d" % i, [128, 512], F32) for i in range(2)]

            wa3 = sb(st, "wa3", [128, 8, D], BF16)
            bcr2 = sb(st, "bcr2", [128, 2, D], F32)

            def ada_dma(g):
                sc.dma("pool", lambda: POOL.dma_start(out=wa3[:], in_=wview(wada_d, g * D, D)), writes=["wa3"])
                if g in (2, 5):
                    bi = 0 if g == 2 else 1
                    sc.dma("sp", lambda: SP.dma_start(out=bcr2[:, 0, :], in_=bc_d[bi:bi + 1, :].partition_broadcast(128)[:, 0, :]), writes=["bcr2"])
                    sc.dma("sp", lambda: SP.dma_start(out=bcr2[:, 1, :], in_=bc_d[2 + bi:3 + bi, :].partition_broadcast(128)[:, 0, :]), writes=["bcr2"])

            ada_order = [2, 3, 4, 5]
            ada_dma(2)

            def load3(cc):
                s_ = cc % 2
                srcs = [wview(win_d, 5 * D + cc * 128, 128), wview(win_d, 6 * D + cc * 128, 128), wview(wlo_d, cc * 128, 128),
                        wview(wno_d, cc * 128, 128)]
                for k in range(4):
                    sc.dma("pool", lambda k=k: POOL.dma_start(out=wsl[s_][k][:], in_=srcs[k]), writes=[("wsl", s_, k)])

            load3(0)
            u_ = 0
            for cc in range(8):
                s_ = cc % 2
                if cc + 1 < 8:
                    load3(cc + 1)
                for p in range(4):
                    z = u_ % 2
                    u_ += 1
                    ts = slice(512 * p, 512 * (p + 1))
                    us = slice(LAT0 + 512 * p, LAT0 + 512 * (p + 1))
                    srcs = [(uT, us, "uT"), (uT, us, "uT"), (hgT, ts, "hgT"), (oT, ts, "oT")]
                    for k in range(4):
                        b = z * 4 + k
                        src, sl_, nm = srcs[k]
                        for kc in range(8):
                            sc.op("pe", lambda b=b, k=k, kc=kc, src=src, sl_=sl_: PE.matmul(ps[:, b, :], lhsT=wsl[s_][k][:, kc, :], rhs=src[:, kc, sl_],
                                                                                            start=(kc == 0), stop=(kc == 7)),
                                  reads=[("wsl", s_, k), (nm, kc)], writes=[("ps", b)])
                    sc.op("act", lambda z=z: ACT.activation(out=gA[z][:], in_=ps[:, z * 4 + 0, :], func=AF.Sigmoid, bias=fmv[:, 3, cc:cc + 1], scale=1.0),
                          reads=[("ps", z * 4 + 0), "fmv"], writes=[("gA", z)])
                    sc.op("act", lambda z=z: ACT.activation(out=gB[z][:], in_=ps[:, z * 4 + 1, :], func=AF.Sigmoid, bias=fmv[:, 4, cc:cc + 1], scale=1.0),
                          reads=[("ps", z * 4 + 1), "fmv"], writes=[("gB", z)])
                    sc.op("dve", lambda z=z: DVE.tensor_tensor(out=t1[z][:], in0=ps[:, z * 4 + 2, :], in1=gA[z][:], op=ALU.mult),
                          reads=[("ps", z * 4 + 2), ("gA", z)], writes=[("t1", z)])
                    sc.op("dve", lambda z=z: DVE.tensor_tensor(out=t2[z][:], in0=ps[:, z * 4 + 3, :], in1=gB[z][:], op=ALU.mult),
                          reads=[("ps", z * 4 + 3), ("gB", z)], writes=[("t2", z)])
                    sc.op("pool", lambda z=z, ts=ts: POOL.tensor_tensor(out=ymT[:, cc, ts], in0=t1[z][:], in1=t2[z][:], op=ALU.add),
                          reads=[("t1", z), ("t2", z)], writes=[("ymT", cc)])
                if cc % 2 == 1:
                    g_ = ada_order[cc // 2]
                    ada_group(g_, ps, wa3, "wa3", bcr2, "bcr2", 0, 1)
                    if cc // 2 + 1 < 4:
                        ada_dma(ada_order[cc // 2 + 1])
            sc.op("dve", lambda: DVE.scalar_tensor_tensor(out=dm[:, 4, :], in0=mraw[:, 3, :, 0], scalar=1.0, in1=fmv[:, 1, :],
                                                          op0=ALU.add, op1=ALU.mult), reads=["mraw", "fmv"], writes=["dm"])
            sc.op("dve", lambda: DVE.tensor_copy(out=dm[:, 5, :], in_=mraw[:, 2, :, 0]), reads=["mraw"], writes=["dm"])
            if dbg:
                sc.dma("sp", lambda: SP.dma_start(out=dbg_d["d_ym"], in_=ymT[:]), reads=[("ymT", k) for k in range(8)])
            sc.barrier()
        if stage_limit <= 3:
            sc.barrier(final=True)
            mid.close()
            return

        mid.close()
        with Stage(arena) as st:
            ps = st.enter_context(nc.psum_tensor("ps4", [128, 8, 512], F32))
            wo = sb(st, "wo", [128, 8, D], BF16)
            wr = [sb(st, "wr%d" % i, [128, 8, D], BF16) for i in range(3)]
            xres = sb(st, "xres", [128, 4, D], F32); x1c = sb(st, "x1c", [128, 4, D], F32)
            u2T = sb(st, "u2T", [128, 8, 512], BF16); h1T = sb(st, "h1T", [128, 32, 512], BF16)
            ost = [sb(st, "ost%d" % i, [128, D], F32) for i in range(4)]
            xnb = [sb(st, "xnq%d" % i, [128, D], BF16) for i in range(4)]
            rl = [sb(st, "rl0", [128, 512], F32)] * 2
            ssy = sb(st, "ssy", [128, 8], F32); vy = sb(st, "vy", [128, 4], F32); ry = sb(st, "ry", [128, 4], F32)
            sc.dma("pool", lambda: POOL.dma_start(out=wo[:], in_=wview(wo_d, 0, D)), writes=["wo"])
            loads = []
            for ck_ in range(4):
                loads += [wview(w1_d, g1 * D, D) for g1 in range(4)]
                loads += [w2_d[g2 * D:(g2 + 1) * D, :].rearrange("(kc p) n -> p kc n", p=128) for g2 in range(4)]
            wst = {"issued": 0, "used": 0}

            def issue_w():
                n = wst["issued"]
                if n < len(loads):
                    wst["issued"] += 1
                    sc.dma("pool", lambda: POOL.dma_start(out=wr[n % 3][:], in_=loads[n]), writes=[("wr", n % 3)])

            def next_w():
                n = wst["used"]
                wst["used"] += 1
                return n % 3

            for _ in range(3):
                issue_w()
            tpv = [ps[:, 6 + h, :].bitcast(BF16).rearrange("p (k t) -> p k t", k=8) for h in range(2)]
            def xres_load(ck):
                for tt in range(4):
                    r0 = (ck * 4 + tt) * 128
                    sc.dma("sp", lambda tt=tt, r0=r0: SP.dma_start(out=xres[:, tt, :], in_=x_d[r0:r0 + 128, :]), writes=[("xres", tt)])

            xres_load(0)
            for ck in range(4):
                def pro(ck, tt):
                    tok = slice((ck * 4 + tt) * 128, (ck * 4 + tt + 1) * 128)
                    yb = [2 * tt, 2 * tt + 1]
                    for half in range(2):
                        b = yb[half]
                        for kc in range(8):
                            sc.op("pe", lambda: PE.matmul(ps[:, b, :], lhsT=ymT[:, kc, tok], rhs=wo[:, kc, half * 512:(half + 1) * 512],
                                                          start=(kc == 0), stop=(kc == 7)), reads=["wo", ("ymT", kc)], writes=[("ps", b)])
                    yield
                    for half in range(2):
                        b = yb[half]
                        sc.op("act", lambda: ACT.activation(out=xnb[tt][:, 0:512], in_=ps[:, b, :], func=AF.Square,
                                                            accum_out=ssy[:, tt * 2 + half:tt * 2 + half + 1]),
                              reads=[("ps", b)], writes=[("xnq", tt), ("ssy", tt)])
                    yield
                    sc.op("dve", lambda: DVE.tensor_tensor(out=vy[:, tt:tt + 1], in0=ssy[:, 2 * tt:2 * tt + 1], in1=ssy[:, 2 * tt + 1:2 * tt + 2], op=ALU.add),
                          reads=[("ssy", tt)], writes=[("vy", tt)])
                    sc.op("dve", lambda: DVE.tensor_scalar(out=vy[:, tt:tt + 1], in0=vy[:, tt:tt + 1], scalar1=1.0 / D, scalar2=EPS, op0=ALU.mult, op1=ALU.add),
                          reads=[("vy", tt)], writes=[("vy", tt)])
                    yield
                    sc.op("pool", lambda: POOL.tensor_tensor(out=ry[:, tt:tt + 1], in0=vy[:, tt:tt + 1], in1=mh[:, 0:1], op=ALU.pow),
                          reads=[("vy", tt), "mh"], writes=[("ry", tt)])
                    yield
                    for half in range(2):
                        b = yb[half]
                        hs = slice(half * 512, (half + 1) * 512)
                        sc.op("dve", lambda: DVE.scalar_tensor_tensor(out=x1c[:, tt, hs], in0=ps[:, b, :], scalar=ry[:, tt:tt + 1], in1=Gb[:, 0, hs],
                                                                      op0=ALU.mult, op1=ALU.mult),
                              reads=[("ps", b), ("ry", tt), ("Gb", 0)], writes=[("x1c", tt)])
                    yield
                    sc.op("pool", lambda: POOL.tensor_tensor(out=x1c[:, tt, :], in0=x1c[:, tt, :], in1=xres[:, tt, :], op=ALU.add),
                          reads=[("x1c", tt), ("xres", tt)], writes=[("x1c", tt)])
                    yield
                    sc.op("act", lambda: ACT.activation(out=xnb[tt][:], in_=x1c[:, tt, :], func=AF.Square, accum_out=ssy[:, tt * 2:tt * 2 + 1]),
                          reads=[("x1c", tt)], writes=[("xnq", tt), ("ssy", tt)])
                    yield
                    sc.op("dve", lambda: DVE.tensor_scalar(out=vy[:, tt:tt + 1], in0=ssy[:, 2 * tt:2 * tt + 1], scalar1=1.0 / D, scalar2=EPS, op0=ALU.mult, op1=ALU.add),
                          reads=[("ssy", tt)], writes=[("vy", tt)])
                    yield
                    sc.op("pool", lambda: POOL.tensor_tensor(out=ry[:, tt:tt + 1], in0=vy[:, tt:tt + 1], in1=mh[:, 0:1], op=ALU.pow),
                          reads=[("vy", tt), "mh"], writes=[("ry", tt)])
                    yield
                    sc.op("act", lambda: ACT.activation(out=xnb[tt][:], in_=x1c[:, tt, :], func=AF.Identity, scale=ry[:, tt:tt + 1]),
                          reads=[("x1c", tt), ("ry", tt)], writes=[("xnq", tt)])
                    yield
                    tvw = ps[:, 2 * tt, :].bitcast(BF16).rearrange("p (k t) -> p k t", k=8)
                    for kc in range(8):
                        sc.op("pe", lambda: PE.transpose(out=tvw[:, kc, :], in_=xnb[tt][:, kc * 128:(kc + 1) * 128], identity=ident[:]),
                              reads=[("xnq", tt), "ident"], writes=[("ps", 2 * tt)])
                    yield
                    for kc in range(8):
                        sc.op("dve", lambda: DVE.tensor_scalar(out=u2T[:, kc, tt * 128:(tt + 1) * 128], in0=tvw[:, kc, :],
                                                               scalar1=dm[:, 4, kc:kc + 1], scalar2=dm[:, 5, kc:kc + 1], op0=ALU.mult, op1=ALU.add),
                              reads=[("ps", 2 * tt), "dm"], writes=[("u2T", tt)])

                if ck == 0:
                    run_skewed([pro(0, tt) for tt in range(4)])
                if ck + 1 < 4:
                    xres_load(ck + 1)
                for g1 in range(4):
                    s_ = next_w()
                    for f8 in range(8):
                        fc = g1 * 8 + f8
                        b = fc % 4
                        for kc in range(8):
                            sc.op("pe", lambda b=b, kc=kc, f8=f8, s_=s_: PE.matmul(ps[:, b, :], lhsT=wr[s_][:, kc, f8 * 128:(f8 + 1) * 128], rhs=u2T[:, kc, :],
                                                                                  start=(kc == 0), stop=(kc == 7)), reads=[("wr", s_)] + [("u2T", t_) for t_ in range(4)], writes=[("ps", b)])
                        z = fc % 2
                        sc.op("act", lambda b=b, z=z: ACT.activation(out=rl[z][:], in_=ps[:, b, :], func=AF.Relu), reads=[("ps", b)], writes=[("rl", 0)])
                        sc.op("dve", lambda b=b, z=z, fc=fc: DVE.tensor_tensor(out=h1T[:, fc, :], in0=ps[:, b, :], in1=rl[z][:], op=ALU.mult),
                              reads=[("ps", b), ("rl", 0)], writes=[("h1T", fc)])
                    issue_w()
                for g2 in range(4):
                    s_ = next_w()
                    for tt in range(4):
                        for half in range(2):
                            b = tt * 2 + half
                            for f8 in range(8):
                                fc = g2 * 8 + f8
                                sc.op("pe", lambda b=b, fc=fc, f8=f8, tt=tt, half=half, s_=s_: PE.matmul(
                                    ps[:, b, :], lhsT=h1T[:, fc, tt * 128:(tt + 1) * 128], rhs=wr[s_][:, f8, half * 512:(half + 1) * 512],
                                    start=(fc == 0), stop=(fc == 31)), reads=[("wr", s_), ("h1T", fc)], writes=[("ps", b)])
                    issue_w()
                def epi(ck, tt):
                    for half in range(2):
                        b = tt * 2 + half
                        sc.op("act", lambda: ACT.activation(out=xnb[tt][:, 0:512], in_=ps[:, b, :], func=AF.Square,
                                                            accum_out=ssy[:, tt * 2 + half:tt * 2 + half + 1]),
                              reads=[("ps", b)], writes=[("xnq", tt), ("ssy", tt)])
                    yield
                    sc.op("dve", lambda: DVE.tensor_tensor(out=vy[:, tt:tt + 1], in0=ssy[:, 2 * tt:2 * tt + 1], in1=ssy[:, 2 * tt + 1:2 * tt + 2], op=ALU.add),
                          reads=[("ssy", tt)], writes=[("vy", tt)])
                    sc.op("dve", lambda: DVE.tensor_scalar(out=vy[:, tt:tt + 1], in0=vy[:, tt:tt + 1], scalar1=1.0 / D, scalar2=EPS, op0=ALU.mult, op1=ALU.add),
                          reads=[("vy", tt)], writes=[("vy", tt)])
                    yield
                    sc.op("pool", lambda: POOL.tensor_tensor(out=ry[:, tt:tt + 1], in0=vy[:, tt:tt + 1], in1=mh[:, 0:1], op=ALU.pow),
                          reads=[("vy", tt), "mh"], writes=[("ry", tt)])
                    yield
                    z = tt
                    for half in range(2):
                        b = tt * 2 + half
                        hs = slice(half * 512, (half + 1) * 512)
                        sc.op("dve", lambda: DVE.scalar_tensor_tensor(out=ost[z][:, hs], in0=ps[:, b, :], scalar=ry[:, tt:tt + 1], in1=Gb[:, 1, hs],
                                                                      op0=ALU.mult, op1=ALU.mult),
                              reads=[("ps", b), ("ry", tt), ("Gb", 1)], writes=[("ost", z)])
                    yield
                    sc.op("pool", lambda: POOL.tensor_tensor(out=ost[z][:], in0=ost[z][:], in1=x1c[:, tt, :], op=ALU.add),
                          reads=[("ost", z), ("x1c", tt)], writes=[("ost", z)])
                    yield
                    r0 = (ck * 4 + tt) * 128
                    sc.dma("sp", lambda: SP.dma_start(out=out_d[r0:r0 + 128, :], in_=ost[z][:]), reads=[("ost", z)])

                gens_ = [epi(ck, tt) for tt in range(4)]
                if ck + 1 < 4:
                    gens_ += [pro(ck + 1, tt) for tt in range(4)]
                run_skewed(gens_)
            sc.barrier(final=True)


def build(stage_limit=99, dbg=False):
    nc0 = bass.Bass("TRN2", target_bir_lowering=False)
    s0 = Sched(nc0, None)
    program(nc0, s0, stage_limit, dbg)
    flags = s0.need
    nc = bass.Bass("TRN2", target_bir_lowering=False)
    s1 = Sched(nc, flags)
    program(nc, s1, stage_limit, dbg)
    return nc


def host_tables(inp):
    f = np.float32
    w_rg = inp["w_rg"][0]; conv_w = inp["conv_w"][0]; rpb = inp["rpb"][0]
    b_ada = inp["b_ada"][0]; g_norm = inp["g_norm"][0]; b_gate = inp["b_gate"][0]; b_rg = inp["b_rg"][0]; lam = inp["lam"][0]

    def fm(v):
        return np.ascontiguousarray(v.reshape(8, 128).T)

    t = {}
    t["badaFM"] = np.ascontiguousarray(b_ada.reshape(6, 8, 128).transpose(2, 0, 1)).astype(f)
    t["bc_rows"] = np.ascontiguousarray(np.stack([b_ada[2 * D:3 * D], b_ada[5 * D:6 * D], g_norm[1], g_norm[3]])).astype(f)
    vecs = [g_norm[0], g_norm[2], inp["conv_b"][0], b_gate[0:D], b_gate[D:2 * D], b_rg[0, 0], b_rg[0, 1], b_rg[1, 0], b_rg[1, 1], lam[0], lam[1]]
    t["fmvec"] = np.ascontiguousarray(np.stack([fm(v) for v in vecs], axis=1)).astype(f)
    wbd = np.zeros((128, 2, 2, 8, 128), f)
    for cc in range(8):
        for hl in range(2):
            wbd[hl * 64:(hl + 1) * 64, :, :, cc, hl * 64:(hl + 1) * 64] = w_rg[:, :, 2 * cc + hl].transpose(2, 0, 1, 3)
    t["wrgBD"] = np.ascontiguousarray(wbd.reshape(128, 32, 128))
    cvd = np.zeros((128, 4, 8, 128), f)
    pidx = np.arange(128)
    for tap in range(4):
        for cc in range(8):
            cvd[pidx, tap, cc, pidx] = conv_w[tap, cc * 128:(cc + 1) * 128]
    t["convD"] = np.ascontiguousarray(cvd.reshape(128, 32, 128))
    krel = np.arange(4)[:, None, None, None]; kcb = np.arange(32)[None, :, None, None]
    qrel = np.arange(12)[None, None, :, None]; qcb = np.arange(16)[None, None, None, :]
    dr = np.clip(krel - qrel + 4 + 7, 0, 14) + 0 * kcb + 0 * qcb
    dc = np.clip(kcb - qcb - 8 + 15, 0, 30) + 0 * krel + 0 * qrel
    rt = rpb[:, dr, dc]
    t["RT"] = np.ascontiguousarray(rt.reshape(16, 128, 192).transpose(1, 0, 2)).astype(f)
    ak = np.zeros((64, 4, 8, 4, 32), f)
    for j in range(4):
        for g in range(8):
            for qr in range(32):
                r0 = min(max(qr - 4, 0), 24)
                for kr_ in range(4):
                    kr = 4 * g + kr_
                    if not (r0 <= kr <= r0 + 7):
                        ak[qr, j, g, kr_, :] = NEG
            for qc_ in range(16):
                qc = 16 * j + qc_
                cs = min(max(qc - 8, 0), 48)
                for kc_ in range(32):
                    kc = 16 * j - 8 + kc_
                    if not ((0 <= kc < 64) and (cs <= kc < cs + 16)):
                        ak[32 + qc_, j, g, :, kc_] = NEG
    t["AK64"] = np.ascontiguousarray(ak.reshape(64, 32, 128))
    eq = np.zeros((64, 32, 64), f)
    for qr in range(32):
        eq[qr, qr, :] = 1.0
    for c_ in range(16):
        eq[32 + c_, :, c_::16] = 1.0
    t["EQ64"] = np.ascontiguousarray(eq.reshape(64, 2048))
    t["ident"] = np.eye(128, dtype=f)
    return t


def make_in_maps(inp):
    f = np.float32
    shared = host_tables(inp)
    for k in ("w_ada", "w_in", "w_lru_out", "w_na_out", "w_o", "w_mlp1", "w_mlp2"):
        shared[k] = np.ascontiguousarray(inp[k][0]).astype(f)
    maps = []
    cctx = inp["c_ctx"].reshape(8, 128).T
    for b in range(8):
        m = dict(shared)
        m["x"] = np.ascontiguousarray(inp["x"][b]).astype(f)
        m["ctx"] = np.ascontiguousarray(inp["ctx"][b]).astype(f)
        m["cc2"] = np.ascontiguousarray(np.stack([inp["c"][b].reshape(8, 128).T, cctx], axis=-1)).astype(f)
        maps.append(m)
    return maps


_NC = {}


def kernel(**inputs):
    inp = {k: np.asarray(v) for k, v in inputs.items()}
    if "nc" not in _NC:
        _NC["nc"] = build()
    maps = make_in_maps(inp)
    res = run_bass_kernel_spmd(_NC["nc"], maps, core_ids=list(range(8)))
    return np.stack([np.asarray(r["out"]) for r in res.results], axis=0).astype(np.float32)
```
